# Optimizing a Trainium2 kernel written in Bass

```python
import math
import jax, jax.numpy as jnp
from jax import lax
import numpy as np

D_MODEL = 1024
BATCH = 16
SEQ = 2048
DEPTH = 2
DEC_BATCH = 8
DEC_SEQ = 16
PAST_LEN = 2048

CHUNK = 64
Q_BLOCK = 128
A_HEADS = 4
A_HALF_DIM = 64
A_QK_DIM = 2 * A_HALF_DIM
A_V_DIM = 2 * A_HALF_DIM
A_WIDTH = A_HEADS * A_V_DIM
LAMBDA_INIT_0 = 0.2
N_BUCKETS = 32
MAX_DISTANCE = 128
B_GROUPS = 4
B_GROUP_DIM = 128
B_WIDTH = B_GROUPS * B_GROUP_DIM
B_CHUNK = 128
L0_SIZES = [A_HEADS * A_QK_DIM, A_HEADS * A_QK_DIM, A_WIDTH, A_WIDTH, B_WIDTH, B_WIDTH, B_WIDTH]
L0_SPLITS = [int(s) for s in np.cumsum(L0_SIZES)[:-1]]
L0_IN = int(sum(L0_SIZES))
C_WIDTH = D_MODEL
CONV_WIDTH = 31
L1_IN = 3 * C_WIDTH
RMS_EPS = 1e-6
LN_EPS = 1e-5

kernel_name = "hybrid_stream_diffattn_gmlp_conformer_step"


def _rmsnorm(x, g):
    xf = x.astype(jnp.float32)
    y = xf * lax.rsqrt(jnp.mean(xf * xf, axis=-1, keepdims=True) + RMS_EPS)
    return (y * g.astype(jnp.float32)).astype(x.dtype)


def _layernorm(x, g, b):
    xf = x.astype(jnp.float32)
    mu = jnp.mean(xf, axis=-1, keepdims=True)
    var = jnp.mean(jnp.square(xf - mu), axis=-1, keepdims=True)
    y = (xf - mu) * lax.rsqrt(var + LN_EPS) * g.astype(jnp.float32) + b.astype(jnp.float32)
    return y.astype(x.dtype)


def _t5_bucket(rel):
    nb = N_BUCKETS // 2
    ret = jnp.where(rel > 0, nb, 0)
    n = jnp.abs(rel)
    max_exact = nb // 2
    nf = jnp.maximum(n, 1).astype(jnp.float32)
    large = max_exact + (jnp.log(nf / max_exact) / math.log(MAX_DISTANCE / max_exact)
                         * (nb - max_exact)).astype(jnp.int32)
    large = jnp.minimum(large, nb - 1)
    return ret + jnp.where(n < max_exact, n, large)


def _diff_attn(q, k, v, q_pos, k_pos, rel_bias, lam):
    s = jnp.einsum('bqhcd,bkhcd->bhcqk', q, k).astype(jnp.float32) * (A_HALF_DIM ** -0.5)
    bias = rel_bias[_t5_bucket(k_pos[None, :] - q_pos[:, None])]
    s = s + jnp.transpose(bias, (2, 0, 1)).astype(jnp.float32)[None, :, None]
    allowed = (k_pos[None, :] // CHUNK) <= (q_pos[:, None] // CHUNK)
    s = jnp.where(allowed[None, None, None], s, -jnp.inf)
    p = jax.nn.softmax(s, axis=-1)
    w = p[:, :, 0] - lam * p[:, :, 1]
    return jnp.einsum('bhqk,bkhe->bqhe', w.astype(v.dtype), v)


def _spatial_gate(v, w_s, b_s):
    bsz, t, g, c = v.shape
    n = -(-t // B_CHUNK)
    pad = n * B_CHUNK - t
    vp = jnp.pad(v, ((0, 0), (0, pad), (0, 0), (0, 0))).reshape(bsz, n, B_CHUNK, g, c)
    causal = jnp.tril(jnp.ones((B_CHUNK, B_CHUNK), dtype=bool))
    wm = jnp.where(causal[None], w_s, 0).astype(v.dtype)
    s = jnp.einsum('gij,bnjgc->bnigc', wm, vp) + b_s.T.astype(v.dtype)[None, None, :, :, None]
    return s.reshape(bsz, n * B_CHUNK, g, c)[:, :t]


def _attn_gmlp_layer(x, cache_k, cache_v, rel_bias, lam, norm_g, w_in, subln_g,
                     gv_ln_g, gv_ln_b, w_s, b_s, w_out):
    bsz, t, _ = x.shape
    xn = _rmsnorm(x, norm_g)
    z = jnp.einsum('btd,de->bte', xn, w_in)
    q, k, v, g_a, u_b, v_b, g_b = jnp.split(z, L0_SPLITS, axis=-1)
    q = q.reshape(bsz, t, A_HEADS, 2, A_HALF_DIM)
    k_new = k.reshape(bsz, t, A_HEADS, A_QK_DIM)
    v_new = v.reshape(bsz, t, A_HEADS, A_V_DIM)
    if cache_k is None:
        past = 0
        k_all, v_all = k_new, v_new
    else:
        past = cache_k.shape[1]
        k_all = jnp.concatenate([cache_k.astype(k_new.dtype), k_new], axis=1)
        v_all = jnp.concatenate([cache_v.astype(v_new.dtype), v_new], axis=1)
    q_pos = past + jnp.arange(t, dtype=jnp.int32)
    k_pos = jnp.arange(past + t, dtype=jnp.int32)
    k_all5 = k_all.reshape(bsz, past + t, A_HEADS, 2, A_HALF_DIM)
    if t % Q_BLOCK == 0:
        n_blk = t // Q_BLOCK
        qb = jnp.moveaxis(q.reshape(bsz, n_blk, Q_BLOCK, A_HEADS, 2, A_HALF_DIM), 1, 0)
        pb = q_pos.reshape(n_blk, Q_BLOCK)
        o = lax.map(lambda a: _diff_attn(a[0], k_all5, v_all, a[1], k_pos, rel_bias, lam), (qb, pb))
        o = jnp.moveaxis(o, 0, 1).reshape(bsz, t, A_HEADS, A_V_DIM)
    else:
        o = _diff_attn(q, k_all5, v_all, q_pos, k_pos, rel_bias, lam)
    o = _rmsnorm(o, subln_g) * (1.0 - LAMBDA_INIT_0)
    out_a = o.reshape(bsz, t, A_WIDTH) * jax.nn.silu(g_a)
    zu = jax.nn.gelu(u_b).reshape(bsz, t, B_GROUPS, B_GROUP_DIM)
    zv = _layernorm(jax.nn.gelu(v_b).reshape(bsz, t, B_GROUPS, B_GROUP_DIM), gv_ln_g, gv_ln_b)
    out_b = (zu * _spatial_gate(zv, w_s, b_s)).reshape(bsz, t, B_WIDTH) * jax.nn.silu(g_b)
    y = x + jnp.einsum('bte,ed->btd', jnp.concatenate([out_a, out_b], axis=-1), w_out)
    return y, k_new, v_new, zv.reshape(bsz, t, B_WIDTH)


def _conv_layer(x, conv_state, norm_g, w_in, w_dw, b_dw, ln_g, ln_b, w_out):
    xn = _rmsnorm(x, norm_g)
    z = jnp.einsum('btd,de->bte', xn, w_in)
    a, b, g = jnp.split(z, [C_WIDTH, 2 * C_WIDTH], axis=-1)
    u = a * jax.nn.sigmoid(b)
    if conv_state is None:
        ctx = jnp.pad(u, ((0, 0), (CONV_WIDTH - 1, 0), (0, 0)))
    else:
        ctx = jnp.concatenate([conv_state.astype(u.dtype), u], axis=1)
    c = lax.conv_general_dilated(ctx, w_dw.astype(ctx.dtype)[:, None, :], window_strides=(1,),
                                 padding='VALID', dimension_numbers=('NWC', 'WIO', 'NWC'),
                                 feature_group_count=C_WIDTH) + b_dw.astype(ctx.dtype)
    c = jax.nn.silu(_layernorm(c, ln_g, ln_b)) * jax.nn.silu(g)
    y = x + jnp.einsum('btc,cd->btd', c, w_out)
    return y, ctx[:, -(CONV_WIDTH - 1):]


def setup_inputs(seed: int = 0) -> dict:
    key = jax.random.key(seed)
    ks = jax.random.split(key, 26)

    def nrm(k, shape, s):
        return jax.random.normal(k, shape, jnp.float32) * s

    return {
        "x_prompt": nrm(ks[0], (BATCH, SEQ, D_MODEL), 1.0),
        "x_sample": nrm(ks[1], (DEC_BATCH, DEC_SEQ, D_MODEL), 1.0),
        "cache_k0": nrm(ks[2], (DEC_BATCH, PAST_LEN, A_HEADS, A_QK_DIM), 1.0),
        "cache_v0": nrm(ks[3], (DEC_BATCH, PAST_LEN, A_HEADS, A_V_DIM), 1.0),
        "state_conv1": nrm(ks[4], (DEC_BATCH, CONV_WIDTH - 1, C_WIDTH), 0.5),
        "rel_bias": nrm(ks[5], (N_BUCKETS, A_HEADS), 0.5),
        "norm_g0": 1.0 + nrm(ks[6], (D_MODEL,), 0.02),
        "w_in0": nrm(ks[7], (D_MODEL, L0_IN), D_MODEL ** -0.5),
        "lambda_q1": nrm(ks[8], (A_HALF_DIM,), 0.1),
        "lambda_k1": nrm(ks[9], (A_HALF_DIM,), 0.1),
        "lambda_q2": nrm(ks[10], (A_HALF_DIM,), 0.1),
        "lambda_k2": nrm(ks[11], (A_HALF_DIM,), 0.1),
        "subln_g0": 1.0 + nrm(ks[12], (A_V_DIM,), 0.02),
        "gv_ln_g0": 1.0 + nrm(ks[13], (B_GROUPS, B_GROUP_DIM), 0.02),
        "gv_ln_b0": nrm(ks[14], (B_GROUPS, B_GROUP_DIM), 0.02),
        "w_s0": nrm(ks[15], (B_GROUPS, B_CHUNK, B_CHUNK), B_CHUNK ** -0.5),
        "b_s0": 1.0 + nrm(ks[16], (B_GROUPS, B_CHUNK), 0.1),
        "w_out0": nrm(ks[17], (A_WIDTH + B_WIDTH, D_MODEL), (A_WIDTH + B_WIDTH) ** -0.5),
        "norm_g1": 1.0 + nrm(ks[18], (D_MODEL,), 0.02),
        "w_in1": nrm(ks[19], (D_MODEL, L1_IN), D_MODEL ** -0.5),
        "w_dw1": nrm(ks[20], (CONV_WIDTH, C_WIDTH), CONV_WIDTH ** -0.5),
        "b_dw1": nrm(ks[21], (C_WIDTH,), 0.02),
        "conv_ln_g1": 1.0 + nrm(ks[22], (C_WIDTH,), 0.02),
        "conv_ln_b1": nrm(ks[23], (C_WIDTH,), 0.02),
        "w_out1": nrm(ks[24], (C_WIDTH, D_MODEL), C_WIDTH ** -0.5),
        "final_g": 1.0 + nrm(ks[25], (D_MODEL,), 0.02),
    }


def reference(x_prompt, x_sample, cache_k0, cache_v0, state_conv1, rel_bias, norm_g0, w_in0,
              lambda_q1, lambda_k1, lambda_q2, lambda_k2, subln_g0, gv_ln_g0, gv_ln_b0, w_s0, b_s0,
              w_out0, norm_g1, w_in1, w_dw1, b_dw1, conv_ln_g1, conv_ln_b1, w_out1, final_g):
    f32 = jnp.float32
    lam = (jnp.exp(jnp.sum(lambda_q1.astype(f32) * lambda_k1.astype(f32)))
           - jnp.exp(jnp.sum(lambda_q2.astype(f32) * lambda_k2.astype(f32))) + LAMBDA_INIT_0)
    xp, xs = x_prompt, x_sample
    for layer in range(DEPTH):
        if layer % 2 == 0:
            xp, k0p, v0p, _ = _attn_gmlp_layer(xp, None, None, rel_bias, lam, norm_g0, w_in0,
                                               subln_g0, gv_ln_g0, gv_ln_b0, w_s0, b_s0, w_out0)
            xs, k0s, v0s, gv0s = _attn_gmlp_layer(xs, cache_k0, cache_v0, rel_bias, lam, norm_g0,
                                                  w_in0, subln_g0, gv_ln_g0, gv_ln_b0, w_s0, b_s0,
                                                  w_out0)
        else:
            xp, c1p = _conv_layer(xp, None, norm_g1, w_in1, w_dw1, b_dw1, conv_ln_g1, conv_ln_b1, w_out1)
            xs, c1s = _conv_layer(xs, state_conv1, norm_g1, w_in1, w_dw1, b_dw1, conv_ln_g1,
                                  conv_ln_b1, w_out1)
    y_prompt = _rmsnorm(xp, final_g)
    y_sample = _rmsnorm(xs, final_g)
    return (y_prompt, y_sample, k0p, v0p, c1p, k0s, v0s, gv0s, c1s)
```

```python
import math
from contextlib import ExitStack

import numpy as np
import ml_dtypes

import concourse.bass as bass
import concourse.mybir as mybir
from concourse.bass_utils import run_bass_kernel_spmd

F32 = mybir.dt.float32
BF16 = mybir.dt.bfloat16
AF = mybir.ActivationFunctionType
ALU = mybir.AluOpType
AX = mybir.AxisListType

NCORES = 8
D = 1024
SEQ = 2048
NTILE = SEQ // 128
NEG = -80000.0
GC = 0.7978845608028654


class Op:
    __slots__ = ("eng", "fn", "deps", "sem", "val", "needed", "dma", "semi")

    def __init__(self, eng, fn, dma):
        self.eng = eng
        self.fn = fn
        self.deps = set()
        self.sem = None
        self.val = 0
        self.needed = False
        self.dma = dma
        self.semi = None


class Sched:
    ENGS = ("pe", "act", "dve", "pool", "sp")
    EXCL = frozenset(["pA", "pB", "pT", "pS0", "pS1", "pO0", "pO1", "pM"])

    def __init__(self, n_dma_sems=None):
        n_dma_sems = n_dma_sems or {"sp": 16, "pool": 40, "pe": 1, "act": 1, "dve": 1}
        self.ops = {e: [] for e in self.ENGS}
        self.lastw = {}
        self.rd = {}
        self.ndma = n_dma_sems
        self.dma_rr = {e: 0 for e in self.ENGS}
        self.dma_last = {}
        self.dma_cnt = {}
        self.all_dma = []

    def add(self, eng, fn, r=(), w=(), dma=False):
        op = Op(eng, fn, dma)
        deps = set()
        for k in r:
            if k in self.lastw:
                deps.add(self.lastw[k])
            if k in self.EXCL:
                rdk = self.rd.get(k)
                if rdk:
                    deps.update(o for en, o in rdk[0].items() if en != eng)
                    deps.update(rdk[1])
        for k in w:
            if k in self.lastw:
                deps.add(self.lastw[k])
            rdk = self.rd.get(k)
            if rdk:
                deps.update(rdk[0].values())
                deps.update(rdk[1])
        for k in r:
            rdk = self.rd.setdefault(k, ({}, []))
            if dma:
                rdk[1].append(op)
            else:
                rdk[0][eng] = op
        for k in w:
            self.lastw[k] = op
            self.rd[k] = ({}, [])
        if dma:
            i = self.dma_rr[eng]
            self.dma_rr[eng] = (i + 1) % self.ndma[eng]
            key = (eng, i)
            op.semi = key
            if key in self.dma_last:
                deps.add(self.dma_last[key])
            self.dma_last[key] = op
            self.dma_cnt[key] = self.dma_cnt.get(key, 0) + 1
            op.val = 16 * self.dma_cnt[key]
            op.needed = True
            self.all_dma.append(op)
        deps.discard(op)
        op.deps = {d for d in deps if d.dma or not (d.eng == "pe" and eng == "pe")}
        for d in op.deps:
            d.needed = True
        self.ops[eng].append(op)
        return op

    def finish(self):
        for e in ("sp", "pool"):
            op = Op(e, None, False)
            op.deps = set(self.all_dma)
            self.ops[e].append(op)

    def emit(self, nc, stack):
        sems = {}
        for e in self.ENGS:
            sems[e] = stack.enter_context(nc.semaphore("s_" + e))
        dsems = {}
        for key in self.dma_cnt:
            dsems[key] = stack.enter_context(nc.semaphore("d_%s%d" % key))
        for e in self.ENGS:
            c = 0
            for op in self.ops[e]:
                if op.dma:
                    op.sem = dsems[op.semi]
                else:
                    op.sem = sems[e]
                    if op.needed:
                        c += 1
                        op.val = c
        block = stack.enter_context(nc.Block())

        def run(ename, eng):
            waited = {}
            for op in self.ops[ename]:
                for d in sorted(op.deps, key=lambda d: d.val):
                    k = id(d.sem)
                    if waited.get(k, 0) < d.val:
                        eng.wait_ge(d.sem, d.val)
                        waited[k] = d.val
                if op.fn is None:
                    continue
                ins = op.fn(eng)
                if op.dma:
                    ins.then_inc(op.sem, 16)
                elif op.needed:
                    ins.then_inc(op.sem, 1)

        block.tensor(lambda e: run("pe", e))
        block.scalar(lambda e: run("act", e))
        block.vector(lambda e: run("dve", e))
        block.gpsimd(lambda e: run("pool", e))
        block.sync(lambda e: run("sp", e))


def build_program(n_prompt_tiles=NTILE, do_sample=True, nseq=2):
    nc = bass.Bass("TRN2", target_bir_lowering=False)
    S = Sched()
    st = ExitStack()

    def din(name, shape, dt=F32):
        return nc.dram_tensor(name, shape, dt, kind="ExternalInput")

    def dout(name, shape):
        return nc.dram_tensor(name, shape, F32, kind="ExternalOutput")

    x_d = din("x", [2, SEQ, D])
    xsm_d = din("xsm", [16, D])
    ck_d = din("ck", [SEQ, 512])
    cv_d = din("cv", [SEQ, 512])
    sc_d = din("sc", [30, D])
    rb_d = din("rel_bias", [32, 4])
    g0_d = din("norm_g0", [D])
    win0_d = din("w_in0", [D, 3584])
    lq1_d = din("lambda_q1", [64])
    lk1_d = din("lambda_k1", [64])
    lq2_d = din("lambda_q2", [64])
    lk2_d = din("lambda_k2", [64])
    subg_d = din("subln_g0", [128])
    gvg_d = din("gv_ln_g0", [512])
    gvb_d = din("gv_ln_b0", [512])
    ws_d = din("w_s0", [4, 128, 128])
    bs_d = din("b_s0", [4, 128])
    wout0_d = din("w_out0", [D, D])
    g1_d = din("norm_g1", [D])
    win1_d = din("w_in1", [D, 3072])
    wdw_d = din("w_dw1", [31, D])
    bdw_d = din("b_dw1", [D])
    lng_d = din("conv_ln_g1", [D])
    lnb_d = din("conv_ln_b1", [D])
    wout1_d = din("w_out1", [D, D])
    fg_d = din("final_g", [D])
    coh_d = din("c_oh", [32, 384])
    cmp_d = din("c_maskp", [128, 128])
    cms_d = din("c_masks", [128, 128])
    cblk_d = din("c_blk", [128, 32])
    ctri_d = din("c_tri", [128, 128])
    cid_d = din("c_idf", [128, 128])

    y_d = dout("y", [2, SEQ, D])
    ys_d = dout("ys", [16, D])
    ko_d = dout("ko", [2, SEQ, 512])
    vo_d = dout("vo", [2, SEQ, 512])
    c1p_d = dout("c1p", [2, 30, D])
    kso_d = dout("kso", [16, 512])
    vso_d = dout("vso", [16, 512])
    gvso_d = dout("gvso", [16, 512])
    c1s_d = dout("c1s", [30, D])

    gd_d = nc.dram_tensor("gd_scr", [4, 128, 384], F32)
    w1s_d = nc.dram_tensor("w1s_scr", [8, 128, 3072], BF16)
    wo1s_d = nc.dram_tensor("wo1s_scr", [4, 128, 2048], BF16)

    def sb(name, shape, dt):
        return st.enter_context(nc.sbuf_tensor(name, shape, dt))

    def ps(name, shape, dt):
        return st.enter_context(nc.psum_tensor(name, shape, dt))

    w0 = sb("w0", [128, 8, 3584], BF16)
    wo0 = sb("wo0", [128, 8, 1024], BF16)
    KT = sb("KT", [128, 4, 17 * 128], BF16)
    VA = sb("VA", [128, 17, 4, 130], BF16)
    w1c = sb("w1c", [128, 2, 3072], BF16)
    wo1c = sb("wo1c", [128, 2, 2048], BF16)
    dgc = sb("dgc", [128, 2, 31, 32], BF16)
    blk31 = sb("blk31", [128, 32], F32)
    wdwT = sb("wdwT", [128, 8, 31], F32)
    idb = sb("idb", [128, 128], BF16)
    idf = sb("idf", [128, 128], F32)
    onesb = sb("onesb", [128, 128], BF16)
    BT = sb("BT", [128, 12, 128], BF16)
    wmT = sb("wmT", [128, 4, 128], BF16)
    cols = sb("cols", [128, 8, 8], F32)
    bsT = sb("bsT", [128, 4], F32)
    fgb = sb("fgb", [128, 1024], F32)
    gvgb = sb("gvgb", [128, 512], F32)
    gvbb = sb("gvbb", [128, 512], F32)
    subgb = sb("subgb", [128, 128], F32)
    nhalf = sb("nhalf", [128, 8], F32)
    stt = sb("stt", [128, 64], F32)
    xg = sb("xg", [128, 4, 1024], F32)
    xs = sb("xs", [128, 1024], BF16)
    xnT = sb("xnT", [128, 2, 8, 128], BF16)
    qkb = sb("qkb", [128, 1024], BF16)
    qT = sb("qT", [128, 2, 4, 128], BF16)
    T = sb("T", [128, 6, 512], F32)
    gates = sb("gates", [128, 4, 512], BF16)
    PT = sb("PT", [128, 2, 2176], BF16)
    uT = sb("uT", [128, 8, 286], BF16)
    yb = sb("yb", [128, 8, 256], BF16)
    ysq = sb("ysq", [128, 2, 256], BF16)
    junk = sb("junk", [128, 512], BF16)
    uo = junk[:, :].bitcast(F32).rearrange("p (c n) -> p c n", c=8)
    pA = ps("pA", [128, 512], F32)
    pB = ps("pB", [128, 512], F32)
    pT_ = ps("pT", [128, 1024], BF16)
    pS = [ps("pS0", [128, 512], F32), ps("pS1", [128, 512], F32)]
    pS3 = None
    pO = [ps("pO0", [128, 512], F32), ps("pO1", [128, 512], F32)]
    pM = ps("pM", [128, 512], F32)
    pS3 = [(pS[0], "pS0"), (pS[1], "pS1"), (pM, "pM"), (pT_[:, :].bitcast(F32), "pT")]

    def PTK(b):
        return [("PT", b, k) for k in range(5)]

    def dma(eng, out, in_, r=(), w=()):
        return S.add(eng, lambda e, o=out, i=in_: e.dma_start(out=o, in_=i), r=r, w=w, dma=True)

    def dma_nc(eng, out, in_, r=(), w=()):
        return S.add(eng, lambda e, o=out, i=in_: e.dma_start(out=o, in_=i, allow_slow_non_contiguous=True), r=r, w=w, dma=True)

    def bcast_rows(t, n, parts=128, off=0):
        return bass.AP(t, off, [[0, parts], [1, n]])

    import os
    _skip = set(os.environ.get("KSKIP", "").split(","))
    if os.environ.get("KSTOP") != "setup" and nseq > 0 and n_prompt_tiles >= 2:
        for k in range(2):
            dma("sp", xg[:, k, :], x_d.ap()[0, k * 128:(k + 1) * 128, :], w=[("xg", k)])
    dma("sp", idf[:, :], cid_d.ap(), w=["idf"])
    S.add("dve", lambda e: e.tensor_copy(idb[:, :], idf[:, :]), r=["idf"], w=["idb"])
    S.add("pool", lambda e: e.memset(onesb[:, :], 1.0), w=["onesb"])
    S.add("pool", lambda e: e.memset(nhalf[:, :], -0.5), w=["nhalf"])
    S.add("pool", lambda e: e.memset(stt[:, 56:57], 1e-5), w=["eps5"])
    S.add("pool", lambda e: e.memset(VA[:, :, :, 128:129], 1.0), w=["VAc"])
    S.add("pool", lambda e: e.memset(VA[:, :, :, 129:130], 0.0), w=["VAc2"])
    dma_nc("sp", cols[:, 0, :], g0_d.ap().rearrange("(c p) -> p c", p=128), w=[("cols", 0)])

    def mid_setup():
        for k, t in enumerate([g0_d, g1_d, bdw_d, lng_d, lnb_d]):
            if k > 0:
                dma_nc("sp", cols[:, k, :], t.ap().rearrange("(c p) -> p c", p=128), w=[("cols", k)])
        S.add("dve", lambda e: e.tensor_copy(cols[:, 5:7, :], cols[:, 3:5, :]), r=[("cols", 3), ("cols", 4)], w=[("cols", 5), ("cols", 6)])
        S.add("dve", lambda e: e.tensor_scalar(cols[:, 3:5, :], cols[:, 5:7, :], 0.5, None, ALU.mult), r=[("cols", 5), ("cols", 6)], w=[("cols", 3), ("cols", 4)])
        S.add("dve", lambda e: e.tensor_scalar(cols[:, 5:7, :], cols[:, 5:7, :], 0.25, None, ALU.mult), r=[("cols", 3), ("cols", 4), ("cols", 5), ("cols", 6)], w=[("cols", 5), ("cols", 6)])
        if "bcast" not in _skip:
            dma_nc("sp", bsT[:, :], bs_d.ap().rearrange("g i -> i g"), w=["bsT"])
            S.add("dve", lambda e: e.tensor_scalar(bsT[:, :], bsT[:, :], 0.25, None, ALU.mult), r=["bsT"], w=["bsT"])
            dma("sp", fgb[:, :], bcast_rows(fg_d, 1024), w=["fgb"])
            dma("sp", gvgb[:, :], bcast_rows(gvg_d, 512), w=["gvgb"])
            dma("sp", gvbb[:, :], bcast_rows(gvb_d, 512), w=["gvbb"])
            dma("sp", subgb[:, :], bcast_rows(subg_d, 128), w=["subgb"])
            S.add("dve", lambda e: e.tensor_scalar(subgb[:, :], subgb[:, :], 0.4, None, ALU.mult), r=["subgb"], w=["subgb"])
        lamt = T[:, 4, 0:256].rearrange("p (a b) -> p a b", a=4)
        for k, t in enumerate([lq1_d, lk1_d, lq2_d, lk2_d]):
            dma("sp", lamt[:, k, :], bcast_rows(t, 64), w=[("T", 4)])
        S.add("dve", lambda e: e.tensor_tensor(lamt[:, 0, :], lamt[:, 0, :], lamt[:, 1, :], ALU.mult), r=[("T", 4)], w=[("T", 4)])
        S.add("dve", lambda e: e.tensor_tensor(lamt[:, 2, :], lamt[:, 2, :], lamt[:, 3, :], ALU.mult), r=[("T", 4)], w=[("T", 4)])
        S.add("dve", lambda e: e.tensor_reduce(stt[:, 0:1], lamt[:, 0, :], AX.X, ALU.add), r=[("T", 4)], w=["lam0"])
        S.add("dve", lambda e: e.tensor_reduce(stt[:, 1:2], lamt[:, 2, :], AX.X, ALU.add), r=[("T", 4)], w=["lam1"])
        S.add("act", lambda e: e.activation(out=stt[:, 2:4], in_=stt[:, 0:2], func=AF.Exp), r=["lam0", "lam1"], w=["lam2"])
        S.add("dve", lambda e: e.tensor_tensor(stt[:, 4:5], stt[:, 3:4], stt[:, 2:3], ALU.subtract), r=["lam2"], w=["lam3"])
        S.add("dve", lambda e: e.tensor_scalar(stt[:, 5:6], stt[:, 4:5], -0.2, None, ALU.add), r=["lam3"], w=["nlam"])

    NLAM = stt[:, 5:6]

    def late_setup(which):
        if "bias" in which and "bias" not in _skip:
            rbt = T[0:32, 0, 0:4]
            rb15 = T[0:32, 0, 4:8]
            rb8 = T[0:32, 0, 8:12]
            oh = T[0:32, 1, 0:384]
            lhb = T[0:32, 2, :]
            dma("sp", rbt, rb_d.ap(), w=[("T", 0)])
            dma("sp", rb15, bass.AP(rb_d, 15 * 4, [[0, 32], [1, 4]]), w=[("T", 0)])
            dma("sp", oh, coh_d.ap(), w=[("T", 1)])
            S.add("dve", lambda e: e.tensor_tensor(rb8, rbt, rb15, ALU.subtract), r=[("T", 0), ("T", 0)], w=[("T", 0)])
            S.add("dve", lambda e: e.tensor_scalar(rb8, rb8, 8.0, None, ALU.mult), r=[("T", 0)], w=[("T", 0)])
            for h in range(4):
                S.add("dve", lambda e, h=h: e.tensor_copy(lhb[:, h * 128:(h + 1) * 128], rb8[:, h:h + 1].to_broadcast([32, 128])), r=[("T", 0)], w=[("T", 2)])
            for h in range(4):
                pbank = pA if h % 2 == 0 else pB
                pk = "pA" if h % 2 == 0 else "pB"
                S.add("pe", lambda e, h=h, pb=pbank: e.matmul(pb[:, 0:384], lhb[:, h * 128:(h + 1) * 128], oh, start=True, stop=True), r=[("T", 2), ("T", 1)], w=[pk])
                gsb = T[:, 3 + (h % 2), 0:384]
                gk = ("T", 3 + (h % 2))
                S.add("dve", lambda e, pb=pbank, g=gsb: e.tensor_copy(g, pb[:, 0:384]), r=[pk], w=[gk])
                dma("sp", gd_d.ap()[h], gsb, r=[gk], w=[("gd", h)])
                tdf = T[:, 5, 0:128]
                tpf = T[:, 5, 128:256]
                dma("sp", tdf, bass.AP(gd_d, h * 128 * 384 + 127, [[383, 128], [1, 128]]), r=[("gd", h)], w=[("T", 5)])
                dma("sp", tpf, bass.AP(gd_d, h * 128 * 384 + 255, [[383, 128], [1, 128]]), r=[("gd", h)], w=[("T", 5)])
                if h == 0:
                    dma("sp", T[:, 5, 256:384], cmp_d.ap(), w=[("T", 5)])
                    dma("sp", T[:, 5, 384:512], cms_d.ap(), w=[("T", 5)])
                S.add("dve", lambda e, h=h, a=tdf: e.tensor_tensor(BT[:, h, :], a, T[:, 5, 256:384], ALU.add), r=[("T", 5), ("T", 5)], w=[("BT", h)])
                S.add("dve", lambda e, h=h, a=tdf: e.tensor_tensor(BT[:, 8 + h, :], a, T[:, 5, 384:512], ALU.add), r=[("T", 5), ("T", 5)], w=[("BT", 8 + h)])
                S.add("dve", lambda e, h=h, a=tpf: e.tensor_copy(BT[:, 4 + h, :], a), r=[("T", 5)], w=[("BT", 4 + h)])

        if "ws" in which and "ws" not in _skip:
            dma("sp", T[:, 0, 0:128], ctri_d.ap(), r=[("T", 0), ("T", 0), ("T", 0)], w=[("T", 0)])
            for g in range(4):
                wsl = T[:, 1 + (g % 2), 0:128]
                wk = ("T", 1 + (g % 2))
                dma("sp", wsl, ws_d.ap()[g], r=[("T", 1), ("T", 2), ("T", 2), ("T", 2), ("T", 2)], w=[wk])
                S.add("pe", lambda e, a=wsl: e.transpose(pM[:, 0:128], a, idf[:, :]), r=[wk, "idf"], w=["pM"])
                S.add("dve", lambda e, g=g: e.scalar_tensor_tensor(wmT[:, g, :], pM[:, 0:128], 0.25, T[:, 0, 0:128], ALU.mult, ALU.mult), r=["pM", ("T", 0)], w=[("wmT", g)])

        if "wdw" in which and "wdw" not in _skip:
            dma("sp", T[0:31, 3, :], wdw_d.ap()[:, 0:512], r=[("T", 3)], w=[("T", 3)])
            dma("sp", T[0:31, 4, :], wdw_d.ap()[:, 512:1024], r=[("T", 4)], w=[("T", 4)])
            for c in range(8):
                src = T[0:31, 3 + c // 4, (c % 4) * 128:(c % 4 + 1) * 128]
                S.add("pe", lambda e, a=src: e.transpose(pM[:, 0:31], a, idf[0:31, 0:31]), r=[("T", 3 + c // 4), "idf"], w=["pM"])
                S.add("dve", lambda e, c=c: e.tensor_scalar(wdwT[:, c, :], pM[:, 0:31], 0.5, None, ALU.mult), r=["pM"], w=[("wdwT", c)])
            dma("sp", blk31[:, :], cblk_d.ap(), w=["blk31"])


    if "weights" not in _skip:
        w0v = win0_d.ap().rearrange("(dt p) e -> p dt e", p=128)
        for j in [1, 2, 0, 3, 4, 5, 6]:
            dma("pool", w0[:, :, j * 512:(j + 1) * 512], w0v[:, :, j * 512:(j + 1) * 512], w=[("w0", j)])
        wo0v = wout0_d.ap().rearrange("(dt p) e -> p dt e", p=128)
        for j in range(2):
            dma("pool", wo0[:, :, j * 512:(j + 1) * 512], wo0v[:, :, j * 512:(j + 1) * 512], w=[("wo0", j)])
    def convert_l1_weights():
        out = []
        w1v = win1_d.ap().rearrange("(dt p) (s e) -> p dt s e", p=128, s=3)
        for c in range(8):
            dst = w1s_d.ap()[c].rearrange("p (dt s e) -> p dt s e", dt=8, s=3)
            for s_ in range(3):
                out.append(lambda c=c, s_=s_, dst=dst: dma("pool", dst[:, :, s_, :], w1v[:, :, s_, c * 128:(c + 1) * 128], w=[("w1s", c, s_)]))
        wo1v = wout1_d.ap().rearrange("(ct p) e -> p ct e", p=128)
        for j in range(4):
            out.append(lambda j=j: dma("pool", wo1s_d.ap()[j].rearrange("p (ct e) -> p ct e", ct=8), wo1v[:, :, j * 256:(j + 1) * 256], w=[("wo1s", j)]))
        return out

    conv_thunks = convert_l1_weights() if "w1s" not in _skip else []

    state = {"xslot": 0, "pbank": 0, "sbank": 0, "obank": 0}
    KL0 = int(os.environ.get("KL0", "9"))
    KL1 = int(os.environ.get("KL1", "9"))

    def next_pbank():
        b = state["pbank"]
        state["pbank"] = 1 - b
        return (pA, "pA") if b == 0 else (pB, "pB")

    def rstd_from_sumsq(ss_ap, out_ap, n, eps, rk, wk, rks=None):
        S.add("pool", lambda e: e.tensor_scalar(out_ap, ss_ap, 1.0 / n, eps, ALU.mult, ALU.add), r=(rks if rks is not None else [rk]), w=[wk])
        S.add("pool", lambda e: e.tensor_tensor(out_ap, out_ap, nhalf[:, 0:out_ap.shape[1]], ALU.pow), r=[wk, "nhalf"], w=[wk])

    def norm_a(xin, xk, scr=None, scr_keys=None, sidx=0):
        cb = {0: 8, 1: 10, 2: 12, 3: 44}[sidx]
        if scr is None:
            scr, scr_keys = xs[:, :], ["xsL", "xsH"]
        ssk, rsk = ("ss", sidx), ("rstd", sidx)
        ssc = stt[:, cb:cb + 1]
        rsc = stt[:, cb + 1:cb + 2]
        S.add("act", lambda e: e.activation(out=scr, in_=xin, func=AF.Square, accum_out=ssc), r=[xk], w=[ssk] + scr_keys)
        rstd_from_sumsq(ssc, rsc, 1024.0, 1e-6, ssk, rsk)
        S.add("dve", lambda e: e.tensor_scalar(scr, xin, rsc, None, ALU.mult), r=[xk, rsk], w=scr_keys)

    def norm_b(gcol_k, dstT, dst_keys, scr=None, scr_keys=None):
        if scr is None:
            scr, scr_keys = xs[:, :], ["xsL", "xsH"]

        def tr(e):
            ins = None
            for dt in range(8):
                ins = e.transpose(pT_[:, dt * 128:(dt + 1) * 128], scr[:, dt * 128:(dt + 1) * 128], idb[:, :])
            return ins
        S.add("pe", tr, r=scr_keys + ["idb"], w=["pT"])
        S.add("dve", lambda e: e.tensor_tensor(dstT, pT_[:, :].rearrange("p (a b) -> p a b", a=8), cols[:, gcol_k, :].unsqueeze(2).to_broadcast([128, 8, 128]), ALU.mult), r=["pT", ("cols", gcol_k)], w=dst_keys)

    def norm_transpose(xin, xk, gcol_k, dstT, dst_keys, tagbase, scr=None, scr_keys=None, sidx=0):
        norm_a(xin, xk, scr, scr_keys, sidx)
        norm_b(gcol_k, dstT, dst_keys, scr, scr_keys)

    def l1_scr(k):
        return yb[:, 4 * k:4 * k + 4, :].rearrange("p a b -> p (a b)"), [("yb", 4 * k + i) for i in range(4)]

    def l1_norm_a(xslot, k):
        scr, sk = l1_scr(k)
        norm_a(xg[:, xslot, :], ("xg", xslot), scr, sk, 1 + k)

    def gate2(psrc, pk, dst, dk, tslot):
        tk = ("T", tslot)
        S.add("act", lambda e: e.activation(out=T[:, tslot, :], in_=psrc, func=AF.Tanh, scale=0.5), r=[pk], w=[tk])
        S.add("dve", lambda e: e.scalar_tensor_tensor(dst, T[:, tslot, :], 1.0, psrc, ALU.add, ALU.mult), r=[tk, pk], w=[dk])

    def gelu2_front(psrc, pk, ta, tb):
        ka, kb = ("T", ta), ("T", tb)
        S.add("act", lambda e: e.activation(out=T[:, ta, :], in_=psrc, func=AF.Copy), r=[pk], w=[ka])
        S.add("act", lambda e: e.activation(out=T[:, tb, :], in_=psrc, func=AF.Square), r=[pk], w=[kb])
        S.add("pool", lambda e: e.tensor_scalar(T[:, tb, :], T[:, tb, :], 0.044715, 1.0, ALU.mult, ALU.add), r=[kb], w=[kb])
        S.add("pool", lambda e: e.tensor_tensor(T[:, tb, :], T[:, tb, :], T[:, ta, :], ALU.mult), r=[ka, kb], w=[kb])

    def gelu2_back(ta, tb, dst, dk):
        ka, kb = ("T", ta), ("T", tb)
        S.add("act", lambda e: e.activation(out=T[:, tb, :], in_=T[:, tb, :], func=AF.Tanh, scale=GC), r=[kb], w=[kb])
        S.add("dve", lambda e: e.scalar_tensor_tensor(dst, T[:, tb, :], 1.0, T[:, ta, :], ALU.add, ALU.mult), r=[ka, kb], w=[dk])

    def l0_parts(xslot, ti, sample, seq, par):
        xin = xg[:, xslot, :]
        xk = ("xg", xslot)
        XT = xnT[:, par, :, :]
        XTK = ("xnT", par)
        QT = qT[:, par, :, :]
        QTK = ("qT", par)
        SGA, SGB, ZU, ZVB = gates[:, 0, :], gates[:, 1, :], gates[:, 2, :], gates[:, 3, :]
        ntk = ti + 1

        def proj(j):
            pb, pk = next_pbank()

            def mm(e):
                ins = None
                for dt in range(8):
                    ins = e.matmul(pb[:, :], XT[:, dt, :], w0[:, dt, j * 512:(j + 1) * 512], start=(dt == 0), stop=(dt == 7))
                return ins
            S.add("pe", mm, r=[XTK, ("w0", j)], w=[pk])
            return pb, pk

        if par == 0:
            a_scr, a_keys, a_sidx = xs[:, :], ["xsL", "xsH"], 0
        else:
            a_scr, a_keys, a_sidx = qkb[:, :], ["qb", "kb"], 3

        def A0():
            norm_a(xin, xk, a_scr, a_keys, a_sidx)

        def A0b():
            norm_b(0, XT, [XTK], a_scr, a_keys)

        def A1():
            pb, pk = proj(1)
            S.add("act", lambda e, pb=pb: e.activation(out=T[:, 0, :], in_=pb[:, :], func=AF.Copy), r=[pk], w=[("T", 0)])
            S.add("dve", lambda e, pb=pb: e.tensor_copy(qkb[:, 512:1024], pb[:, :]), r=[pk], w=["kb"])
            if sample:
                dma("sp", kso_d.ap(), T[0:16, 0, :], r=[("T", 0)])
            else:
                dma("sp", ko_d.ap()[seq, ti * 128:(ti + 1) * 128, :], T[:, 0, :], r=[("T", 0)])
            pb, pk = proj(2)
            S.add("act", lambda e, pb=pb: e.activation(out=T[:, 1, :], in_=pb[:, :], func=AF.Copy), r=[pk], w=[("T", 1)])
            S.add("dve", lambda e, pb=pb: e.tensor_copy(VA[:, ti, :, 0:128], pb[:, :].rearrange("p (h d) -> p h d", h=4)), r=[pk], w=[("VA", ti)])
            if sample:
                dma("sp", vso_d.ap(), T[0:16, 1, :], r=[("T", 1)])
            else:
                dma("sp", vo_d.ap()[seq, ti * 128:(ti + 1) * 128, :], T[:, 1, :], r=[("T", 1)])
            pb, pk = proj(0)
            S.add("act", lambda e, pb=pb: e.activation(out=qkb[:, 0:512], in_=pb[:, :], func=AF.Copy), r=[pk], w=["qb"])

            def trqk(e):
                ins = None
                for a in range(8):
                    ins = e.transpose(pT_[:, a * 128:(a + 1) * 128], qkb[:, a * 128:(a + 1) * 128], idb[:, :])
                return ins
            S.add("pe", trqk, r=["qb", "kb", "idb"], w=["pT"])
            S.add("dve", lambda e: e.tensor_copy(QT, pT_[:, 0:512].rearrange("p (h t) -> p h t", h=4)), r=["pT"], w=[QTK])
            S.add("dve", lambda e: e.tensor_copy(KT[:, :, ti * 128:(ti + 1) * 128], pT_[:, 512:1024].rearrange("p (h t) -> p h t", h=4)), r=["pT"], w=[("KT", ti)])

        def G0():
            pb, pk = proj(3)
            gate2(pb[:, :], pk, SGA, "sga", 2)

        def G1():
            pb, pk = proj(4)
            gelu2_front(pb[:, :], pk, 2, 3)

        def G1b():
            gelu2_back(2, 3, ZU, "zu")

        def G2():
            pb, pk = proj(5)
            gelu2_front(pb[:, :], pk, 0, 1)

        def G2b():
            gelu2_back(0, 1, T[:, 4, :], ("T", 4))

        def G2c():
            GV = T[:, 4, :]
            gv3 = GV.rearrange("p (g c) -> p g c", g=4)
            S.add("dve", lambda e: e.tensor_reduce(stt[:, 16:20], gv3, AX.X, ALU.add), r=[("T", 4)], w=["gvs"])
            S.add("pool", lambda e: e.tensor_tensor(T[:, 3, :], GV, GV, ALU.mult), r=[("T", 4)], w=[("T", 3)])
            S.add("dve", lambda e: e.tensor_reduce(stt[:, 20:24], T[:, 3, :].rearrange("p (g c) -> p g c", g=4), AX.X, ALU.add), r=[("T", 3)], w=["gvq"])
            S.add("pool", lambda e: e.tensor_scalar(stt[:, 16:20], stt[:, 16:20], 1.0 / 128, None, ALU.mult), r=["gvs"], w=["gvs"])
            S.add("pool", lambda e: e.tensor_tensor(stt[:, 24:28], stt[:, 16:20], stt[:, 16:20], ALU.mult), r=["gvs"], w=["gvm2"])
            S.add("dve", lambda e: e.scalar_tensor_tensor(stt[:, 20:24], stt[:, 20:24], 1.0 / 128, stt[:, 24:28], ALU.mult, ALU.subtract), r=["gvq", "gvm2"], w=["gvq"])
            S.add("pool", lambda e: e.tensor_scalar(stt[:, 20:24], stt[:, 20:24], 1.0, 4e-5, ALU.mult, ALU.add), r=["gvq"], w=["gvq"])
            S.add("pool", lambda e: e.tensor_tensor(stt[:, 20:24], stt[:, 20:24], nhalf[:, 0:4], ALU.pow), r=["gvq", "nhalf"], w=["gvq"])

        def G2d():
            GV = T[:, 4, :]
            for g in range(4):
                S.add("dve", lambda e, g=g: e.tensor_scalar(T[:, 3, g * 128:(g + 1) * 128], GV[:, g * 128:(g + 1) * 128], stt[:, 16 + g:17 + g], stt[:, 20 + g:21 + g], ALU.subtract, ALU.mult), r=[("T", 4), "gvs", "gvq"], w=[("T", 3)])
            S.add("pool", lambda e: e.tensor_tensor(T[:, 3, :], T[:, 3, :], gvgb[:, :], ALU.mult), r=[("T", 3), "gvgb"], w=[("T", 3)])
            S.add("pool", lambda e: e.tensor_tensor(T[:, 3, :], T[:, 3, :], gvbb[:, :], ALU.add), r=[("T", 3), "gvbb"], w=[("T", 3)])
            S.add("dve", lambda e: e.tensor_copy(ZVB, T[:, 3, :]), r=[("T", 3)], w=["zvb"])
            if sample:
                dma("sp", gvso_d.ap(), T[0:16, 3, :], r=[("T", 3)])

        def G3():
            pb, pk = proj(6)
            gate2(pb[:, :], pk, SGB, "sgb", 2)

        g4 = {}

        def G4():
            pb, pk = next_pbank()
            g4["pb"], g4["pk"] = pb, pk

            def spm(e):
                ins = None
                for g in range(4):
                    ins = e.matmul(pb[:, g * 128:(g + 1) * 128], wmT[:, g, :], ZVB[:, g * 128:(g + 1) * 128], start=True, stop=True)
                return ins
            S.add("pe", spm, r=["zvb"] + [("wmT", g) for g in range(4)], w=[pk])

        def G4b():
            pb, pk = g4["pb"], g4["pk"]
            for g in range(4):
                S.add("dve", lambda e, g=g: e.scalar_tensor_tensor(T[:, 3, g * 128:(g + 1) * 128], pb[:, g * 128:(g + 1) * 128], bsT[:, g:g + 1], ZU[:, g * 128:(g + 1) * 128], ALU.add, ALU.mult), r=[pk, "bsT", "zu", ("T", 3)], w=[("T", 3)])
            S.add("pool", lambda e: e.tensor_tensor(xs[:, 512:1024], T[:, 3, :], SGB, ALU.mult), r=[("T", 3), "sgb"], w=["xsH"])

        hc_state = {}

        def QK(h, c):
            ptb = (2 * h + c) % 2
            ptk = ("PT", ptb)
            for g0 in range(0, ntk, 4):
                g1 = min(ntk, g0 + 4)
                sbk = state["sbank"]
                state["sbank"] = (sbk + 1) % 4
                psb, psk = pS3[sbk]

                def qk(e, g0=g0, g1=g1, psb=psb):
                    ins = None
                    for j in range(g0, g1):
                        near = None
                        if j == ti:
                            near = (8 if sample else 0) + h
                        elif j == ti - 1:
                            near = 4 + h
                        o = psb[:, (j - g0) * 128:(j - g0 + 1) * 128]
                        ins = e.matmul(o, KT[c * 64:(c + 1) * 64, h, j * 128:(j + 1) * 128], QT[c * 64:(c + 1) * 64, h, :], start=True, stop=(near is None))
                        if near is not None:
                            ins = e.matmul(o, idb[:, :], BT[:, near, :], start=False, stop=True)
                    return ins
                S.add("pe", qk, r=[QTK, "idb"] + [("KT", j) for j in range(g0, g1)] + [("BT", k) for k in range(12)], w=[psk])
                S.add("act", lambda e, g0=g0, g1=g1, psb=psb: e.activation(out=PT[:, ptb, g0 * 128:g1 * 128], in_=psb[:, 0:(g1 - g0) * 128], func=AF.Exp, scale=0.125), r=[psk], w=[("PT", ptb, g0 // 4)])

        def PV(h, c):
            ptb = (2 * h + c) % 2
            ptk = ("PT", ptb)
            if c == 0:
                ob = state["obank"]
                state["obank"] = 1 - ob
                hc_state[h] = ob
            ob = hc_state[h]
            po, pok = pO[ob], "pO%d" % ob

            for g0 in range(0, ntk, 4):
                g1 = min(ntk, g0 + 4)

                def pv(e, g0=g0, g1=g1):
                    ins = None
                    for j in range(g0, g1):
                        ins = e.matmul(po[:, c * 130:(c + 1) * 130], PT[:, ptb, j * 128:(j + 1) * 128], VA[:, j, h, :], start=(j == 0), stop=(j == ntk - 1))
                    return ins
                S.add("pe", pv, r=[("PT", ptb, g0 // 4), "VAc", "VAc2"] + [("VA", j) for j in range(g0, g1)], w=[pok])

        def POST(h):
            ob = hc_state[h]
            po, pok = pO[ob], "pO%d" % ob
            S.add("dve", lambda e: e.reciprocal(stt[:, 32:34], po[:, 128:259:130]), r=[pok], w=["r01"])
            S.add("dve", lambda e: e.tensor_tensor(stt[:, 33:34], stt[:, 33:34], NLAM, ALU.mult), r=["r01", "nlam"], w=["r01"])
            oh_ = T[:, 5, h * 128:(h + 1) * 128]
            S.add("dve", lambda e: e.tensor_scalar(oh_, po[:, 0:128], stt[:, 32:33], None, ALU.mult), r=[pok, "r01"], w=[("T", 5)])
            S.add("dve", lambda e: e.scalar_tensor_tensor(oh_, po[:, 130:258], stt[:, 33:34], oh_, ALU.mult, ALU.add), r=[pok, "r01", ("T", 5)], w=[("T", 5)])
            S.add("act", lambda e: e.activation(out=junk[:, 0:128], in_=oh_, func=AF.Square, accum_out=stt[:, 36 + h:37 + h]), r=[("T", 5)], w=[("oss", h), "junk"])

        def FIN():
            rstd_from_sumsq(stt[:, 36:40], stt[:, 40:44], 128.0, 1e-6, None, "ors", rks=[("oss", h) for h in range(4)])
            o3 = T[:, 5, :].rearrange("p (h d) -> p h d", h=4)
            S.add("dve", lambda e: e.tensor_tensor(o3, o3, stt[:, 40:44].unsqueeze(2).to_broadcast([128, 4, 128]), ALU.mult), r=[("T", 5), "ors"], w=[("T", 5)])
            S.add("dve", lambda e: e.tensor_tensor(o3, o3, subgb[:, :].unsqueeze(1).to_broadcast([128, 4, 128]), ALU.mult), r=[("T", 5), "subgb"], w=[("T", 5)])
            S.add("pool", lambda e: e.tensor_tensor(xs[:, 0:512], T[:, 5, :], SGA, ALU.mult), r=[("T", 5), "sga"], w=["xsL"])

        def O0():
            def tro(e):
                ins = None
                for a in range(8):
                    ins = e.transpose(pT_[:, a * 128:(a + 1) * 128], xs[:, a * 128:(a + 1) * 128], idb[:, :])
                return ins
            S.add("pe", tro, r=["xsL", "xsH", "idb"], w=["pT"])
            S.add("dve", lambda e: e.tensor_copy(XT, pT_[:, :].rearrange("p (a b) -> p a b", a=8)), r=["pT"], w=[XTK])
            for j in range(2):
                pb, pk = next_pbank()

                def mm(e, pb=pb, j=j):
                    ins = None
                    for et in range(8):
                        ins = e.matmul(pb[:, :], XT[:, et, :], wo0[:, et, j * 512:(j + 1) * 512], start=(et == 0), stop=(et == 7))
                    return ins
                S.add("pe", mm, r=[XTK, ("wo0", j)], w=[pk])
                S.add("dve", lambda e, pb=pb, j=j: e.tensor_tensor(xin[:, j * 512:(j + 1) * 512], xin[:, j * 512:(j + 1) * 512], pb[:, :], ALU.add), r=[pk, xk], w=[xk])

        hcs = [(h, c) for h in range(4) for c in range(2)]
        H = []
        H.append(lambda: QK(0, 0))
        for n, (h, c) in enumerate(hcs):
            if n + 1 < 8:
                H.append(lambda n=n: QK(*hcs[n + 1]))
            H.append(lambda h=h, c=c: PV(h, c))
            if c == 1:
                H.append(lambda h=h: POST(h))
        H.append(FIN)
        return {"A": [A0, A0b, A1], "G": [G0, G1, G2, G1b, G2b, G2c, G2d, G3, G4, G4b], "H": H, "O": [O0], "ntk": ntk}

    def emit_A(parts):
        for p in parts:
            for f in p["A"]:
                f()

    def emit_rest(parts, after_first_O=None, extra=None):
        pending = None
        for p in parts:
            H, G = list(p["H"]), list(p["G"])
            ntk_small = p.get("ntk", 99) <= 6
            G.pop(0)()
            for n, f in enumerate(H):
                f()
                if n == 2 and pending is not None:
                    pending()
                    pending = None
                    if after_first_O is not None:
                        after_first_O()
                        after_first_O = None
                if (n % 2 == 1 or ntk_small) and G:
                    G.pop(0)()
                if extra:
                    for _ in range(2):
                        if extra:
                            extra.pop(0)()
            while G:
                G.pop(0)()
            if pending is not None:
                pending()
            pending = p["O"][0]
        return pending

    def prefetch_w1():
        for c in range(2):
            dma("sp", w1c[:, c % 2, :], w1s_d.ap()[c], r=[("w1s", c, 0), ("w1s", c, 1), ("w1s", c, 2)], w=[("w1c", c % 2)])

    def l1_group(xslots, nt, sample, seq, tile0, first, last, pending_O=None, w_prefetched=False, norm_a_done=(), early_hook=None, per_c_hook=None):
        Tn = 128 * nt
        if KL1 <= 0:
            return
        XN = PT[:, 0, 0:2048].rearrange("p (a b) -> p a b", a=8)
        SG = PT[:, 1, 0:2048].rearrange("p (a b) -> p a b", a=8)
        if first and not sample:
            S.add("pool", lambda e: e.memset(uT[:, :, 0:30], 0.0), r=[("uT", c) for c in range(8)], w=[("uTh", c) for c in range(8)])
        elif sample:
            stv = T[0:30, 0:2, :].rearrange("p a b -> p (a b)")
            dma("sp", stv, sc_d.ap(), r=[("T", 0), ("T", 1)], w=[("T", 0), ("T", 1)])
            for c in range(8):
                S.add("pe", lambda e, c=c: e.transpose(pM[:, 0:30], stv[:, c * 128:(c + 1) * 128], idf[0:30, 0:30]), r=[("T", 0), ("T", 1), "idf"], w=["pM"])
                S.add("dve", lambda e, c=c: e.tensor_scalar(uT[:, c, 0:30], pM[:, 0:30], 2.0, None, ALU.mult), r=["pM", ("uT", c)], w=[("uTh", c)])
            dma("sp", c1s_d.ap()[0:14, :], sc_d.ap()[16:30, :])
        else:
            S.add("pool", lambda e: e.tensor_copy(uT[:, :, 0:30], uT[:, :, 256:286]), r=[("uT", c) for c in range(8)], w=[("uTh", c) for c in range(8)])
        for k, xslot in enumerate(xslots):
            if k == len(xslots) - 1 and pending_O is not None:
                pending_O()
            if k not in norm_a_done:
                l1_norm_a(xslot, k)
            scr, sk = l1_scr(k)
            norm_b(1, XN[:, :, k * 128:(k + 1) * 128], [*PTK(0)], scr, sk)

        if KL1 <= 1:
            return
        banks = [(pA, "pA", pB, "pB", pB[:, 256:512], "pB"), (pS[0], "pS0", pS[1], "pS1", pS[1][:, 256:512], "pS1")]

        def loadw(c):
            dma("sp", w1c[:, c % 2, :], w1s_d.ap()[c], r=[("w1s", c, 0), ("w1s", c, 1), ("w1s", c, 2)], w=[("w1c", c % 2)])

        def projc(c):
            pab, pabk, pg, pgk, _, _ = banks[c % 2]
            wv = w1c[:, c % 2, :].rearrange("p (dt s e) -> p dt s e", dt=8, s=3)

            def mm_ab(e):
                ins = None
                for s_, dst in ((1, pab[:, 256:256 + Tn]), (0, pab[:, 0:Tn])):
                    for dt in range(8):
                        ins = e.matmul(dst, wv[:, dt, s_, :], XN[:, dt, 0:Tn], start=(dt == 0), stop=(dt == 7))
                return ins

            def mm_g(e):
                ins = None
                for dt in range(8):
                    ins = e.matmul(pg[:, 0:Tn], wv[:, dt, 2, :], XN[:, dt, 0:Tn], start=(dt == 0), stop=(dt == 7))
                return ins
            S.add("pe", mm_ab, r=[("w1c", c % 2), *PTK(0)], w=[pabk])
            S.add("pe", mm_g, r=[("w1c", c % 2), *PTK(0)], w=[pgk])

        def dgbuild(c):
            S.add("pool", lambda e: e.tensor_tensor(dgc[:, c % 2, :, :], blk31[:, :].unsqueeze(1).to_broadcast([128, 31, 32]), wdwT[:, c, :].unsqueeze(2).to_broadcast([128, 31, 32]), ALU.mult), r=["blk31", ("wdwT", c)], w=[("dgc", c % 2)])

        def elem_u(c):
            pab, pabk, pg, pgk, _, _ = banks[c % 2]
            ta = T[:, 0, 0:Tn]
            S.add("act", lambda e: e.activation(out=ta, in_=pab[:, 256:256 + Tn], func=AF.Tanh, scale=0.5), r=[pabk], w=[("T", 0)])
            S.add("dve", lambda e: e.scalar_tensor_tensor(uT[:, c, 30:30 + Tn], ta, 1.0, pab[:, 0:Tn], ALU.add, ALU.mult), r=[("T", 0), pabk, ("uTh", c)], w=[("uT", c)])
            if last:
                n_o = 16 if sample else 30
                c0 = 0 if sample else Tn - 30
                S.add("dve", lambda e: e.scalar_tensor_tensor(uo[:, c, 0:n_o], ta[:, c0:c0 + n_o], 1.0, pab[:, c0:c0 + n_o], ALU.add, ALU.mult), r=[("T", 0), pabk], w=[("uo", c), "junk"])
            tg = T[:, 1, 0:Tn]
            S.add("act", lambda e: e.activation(out=tg, in_=pg[:, 0:Tn], func=AF.Tanh, scale=0.5), r=[pgk], w=[("T", 1)])
            S.add("dve", lambda e: e.scalar_tensor_tensor(SG[:, c, 0:Tn], tg, 1.0, pg[:, 0:Tn], ALU.add, ALU.mult), r=[("T", 1), pgk], w=[*PTK(1)])

        def conv(c):
            _, _, _, _, pc, pck = banks[c % 2]

            def mm(e):
                ins = None
                for w_ in range(31):
                    for b in range(4):
                        ins = e.matmul(pc[32 * b:32 * b + 32, 0:Tn], dgc[32 * b:32 * b + 32, c % 2, w_, :], uT[32 * b:32 * b + 32, c, w_:w_ + Tn], start=(w_ == 0), stop=(w_ == 30), tile_position=(32 * b, 32 * b))
                return ins
            S.add("pe", mm, r=[("dgc", c % 2), ("uT", c), ("uTh", c)], w=[pck])

        def evac_y(c):
            _, _, _, _, pc, pck = banks[c % 2]
            S.add("act", lambda e: e.activation(out=yb[:, c, 0:Tn], in_=pc[:, 0:Tn], func=AF.Identity, bias=cols[:, 2, c:c + 1]), r=[pck, ("cols", 2)], w=[("yb", c)])
            S.add("act", lambda e: e.activation(out=ysq[:, c % 2, 0:Tn], in_=pc[:, 0:Tn], func=AF.Square, bias=cols[:, 2, c:c + 1]), r=[pck, ("cols", 2)], w=[("ysq", c % 2)])

        def stats(c):
            def mm(e):
                e.matmul(pM[:, 0:Tn], onesb[:, :], yb[:, c, 0:Tn], start=(c == 0), stop=(c == 7))
                return e.matmul(pO[0][:, 0:Tn], onesb[:, :], ysq[:, c % 2, 0:Tn], start=(c == 0), stop=(c == 7))
            S.add("pe", mm, r=[("yb", c), ("ysq", c % 2), "onesb"], w=["pM", "pO0"])

        if not w_prefetched:
            loadw(0)
            loadw(1)
        if not state.get("dg_primed"):
            dgbuild(0)
            dgbuild(1)
            state["dg_primed"] = True
        projc(0)
        for c in range(8):
            if c == 5 and early_hook is not None:
                early_hook()
            if per_c_hook is not None:
                per_c_hook(c)
            if c + 1 < 8:
                projc(c + 1)
            elem_u(c)
            if c + 2 < 8:
                loadw(c + 2)
            conv(c)
            dgbuild((c + 2) % 8)
            evac_y(c)
            if c >= 1:
                stats(c - 1)
        stats(7)
        S.add("act", lambda e: e.activation(out=stt[:, 58:59], in_=stt[:, 56:57], func=AF.Sqrt), r=["eps5"], w=["sqdummy"])
        if per_c_hook is not None:
            per_c_hook(8)
        if KL1 <= 2:
            return
        if last:
            n_o = 16 if sample else 30
            for half in range(2):
                pbh, pbk = (pA, "pA") if half == 0 else (pB, "pB")

                def tru(e, half=half, pbh=pbh):
                    ins = None
                    for cc in range(4):
                        c = half * 4 + cc
                        ins = e.transpose(pbh[0:n_o, cc * 128:(cc + 1) * 128], uo[:, c, 0:n_o], idf[:, :])
                    return ins
                S.add("pe", tru, r=[("uo", c) for c in range(8)] + ["idf", "junk"], w=[pbk])
                S.add("dve", lambda e, half=half, pbh=pbh: e.tensor_scalar(T[0:n_o, 2 + half, :], pbh[0:n_o, :], 0.5, None, ALU.mult), r=[pbk], w=[("T", 2 + half)])
                if sample:
                    dma("sp", c1s_d.ap()[14:30, half * 512:(half + 1) * 512], T[0:16, 2 + half, :], r=[("T", 2 + half)])
                else:
                    dma("sp", c1p_d.ap()[seq, :, half * 512:(half + 1) * 512], T[0:30, 2 + half, :], r=[("T", 2 + half)])
        pt0f = PT[:, 0, 0:1536].bitcast(F32)
        M_, R_, Q_ = pt0f[:, 0:Tn], pt0f[:, 256:256 + Tn], pt0f[:, 512:512 + Tn]

        def p2_stats():
            pass
            S.add("act", lambda e: e.activation(out=M_, in_=pM[:, 0:Tn], func=AF.Copy, scale=1.0 / 1024), r=["pM"], w=[*PTK(0)])
            S.add("pool", lambda e: e.tensor_tensor(Q_, M_, M_, ALU.mult), r=[*PTK(0)], w=["mrQ"])
            S.add("dve", lambda e: e.scalar_tensor_tensor(R_, pO[0][:, 0:Tn], 1.0 / 1024, Q_, ALU.mult, ALU.subtract), r=["pO0", "mrQ"], w=[*PTK(0)])
            S.add("act", lambda e: e.activation(out=R_, in_=R_, func=AF.Sqrt, bias=stt[:, 56:57]), r=[*PTK(0), "eps5"], w=[*PTK(0)])
            S.add("act", lambda e: e.activation(out=stt[:, 59:60], in_=stt[:, 56:57], func=AF.Tanh), r=["eps5"], w=["thdummy"])
            S.add("dve", lambda e: e.reciprocal(R_, R_), r=[*PTK(0)], w=[*PTK(0)])

        def p2_norm(c0):
            if True:
                cs = list(range(c0, c0 + 4))
                sets = {c: (T[:, 2 + (c % 4), 0:Tn], T[:, 2 + (c % 4), 256:256 + Tn], ("T", 2 + (c % 4))) for c in cs}
                for c in cs:
                    t1, t2, k1 = sets[c]
                    S.add("dve", lambda e, c=c, t1=t1: e.tensor_tensor(t1, yb[:, c, 0:Tn], M_, ALU.subtract), r=[("yb", c), *PTK(0)], w=[k1])
                for c in cs:
                    t1, t2, k1 = sets[c]
                    S.add("pool", lambda e, t1=t1: e.tensor_tensor(t1, t1, R_, ALU.mult), r=[k1, *PTK(0)], w=[k1])
                for c in cs:
                    t1, t2, k1 = sets[c]
                    S.add("act", lambda e, c=c, t1=t1, t2=t2: e.activation(out=t2, in_=t1, func=AF.Tanh, scale=cols[:, 3, c:c + 1], bias=cols[:, 4, c:c + 1]), r=[k1, ("cols", 3), ("cols", 4)], w=[k1])
                    S.add("act", lambda e, c=c, t1=t1: e.activation(out=t1, in_=t1, func=AF.Identity, scale=cols[:, 5, c:c + 1], bias=cols[:, 6, c:c + 1]), r=[k1, ("cols", 5), ("cols", 6)], w=[k1])
                for c in cs:
                    t1, t2, k1 = sets[c]
                    S.add("dve", lambda e, t1=t1, t2=t2: e.scalar_tensor_tensor(t1, t2, 1.0, t1, ALU.add, ALU.mult), r=[k1], w=[k1])
                for c in cs:
                    t1, t2, k1 = sets[c]
                    S.add("pool", lambda e, c=c, t1=t1: e.tensor_tensor(yb[:, c, 0:Tn], t1, SG[:, c, 0:Tn], ALU.mult), r=[k1, *PTK(1)], w=[("yb", c)])

        def p2_tail():
            for j in range(4):
                dma("sp", wo1c[:, j % 2, :], wo1s_d.ap()[j], r=[("wo1s", j)], w=[("wo1c", j % 2)])
                pb, pk = next_pbank()
                wv = wo1c[:, j % 2, :].rearrange("p (ct e) -> p ct e", ct=8)

                def mm(e, pb=pb, wv=wv):
                    ins = None
                    for k in range(nt):
                        for ct in range(8):
                            ins = e.matmul(pb[:, k * 256:(k + 1) * 256], yb[:, ct, k * 128:(k + 1) * 128], wv[:, ct, :], start=(ct == 0), stop=(ct == 7))
                    return ins
                S.add("pe", mm, r=[("wo1c", j % 2)] + [("yb", c) for c in range(8)], w=[pk])
                for k, xslot in enumerate(xslots):
                    xv = xg[:, xslot, j * 256:(j + 1) * 256]
                    S.add("dve", lambda e, pb=pb, k=k, xv=xv: e.tensor_tensor(xv, xv, pb[:, k * 256:(k + 1) * 256], ALU.add), r=[pk, ("xg", xslot)], w=[("xg", xslot)])
            for k, xslot in enumerate(xslots):
                xv = xg[:, xslot, :]
                xk = ("xg", xslot)
                S.add("act", lambda e, xv=xv: e.activation(out=junk[:, :], in_=xv[:, 0:512], func=AF.Square, accum_out=stt[:, 48:49]), r=[xk], w=["fs0", "junk"])
                S.add("act", lambda e, xv=xv: e.activation(out=junk[:, :], in_=xv[:, 512:1024], func=AF.Square, accum_out=stt[:, 49:50]), r=[xk], w=["fs1", "junk"])
                S.add("pool", lambda e: e.tensor_tensor(stt[:, 50:51], stt[:, 48:49], stt[:, 49:50], ALU.add), r=["fs0", "fs1"], w=["fs"])
                rstd_from_sumsq(stt[:, 50:51], stt[:, 51:52], 1024.0, 1e-6, "fs", "frs")
                ov = T[:, 4:6, :].rearrange("p a b -> p (a b)")
                S.add("dve", lambda e, xv=xv, ov=ov: e.scalar_tensor_tensor(ov, xv, stt[:, 51:52], fgb[:, :], ALU.mult, ALU.mult), r=[xk, "frs", "fgb"], w=[("T", 4), ("T", 5)])
                if sample:
                    dma("sp", ys_d.ap(), ov[0:16, :], r=[("T", 4), ("T", 5)])
                else:
                    t = tile0 + k
                    dma("sp", y_d.ap()[seq, t * 128:(t + 1) * 128, :], ov, r=[("T", 4), ("T", 5)])


        return {"stats": p2_stats, "norm": [lambda: p2_norm(0), lambda: p2_norm(4)], "tail": p2_tail}

    def load_x(slot, seq, ti):
        dma("sp", xg[:, slot, :], x_d.ap()[seq, ti * 128:(ti + 1) * 128, :], w=[("xg", slot)])

    import os
    groups = []
    if os.environ.get("KSTOP") != "setup":
        for seq in range(nseq):
            for t0 in range(0, n_prompt_tiles, 2):
                groups.append((seq, t0))
    else:
        do_sample = False

    def slots_of(gi):
        return [(2 * gi) % 4, (2 * gi + 1) % 4]

    def load_group(gi):
        seq, t0 = groups[gi]
        sl = slots_of(gi)
        load_x(sl[0], seq, t0)
        load_x(sl[1], seq, t0 + 1)

    def make_parts(gi):
        seq, t0 = groups[gi]
        sl = slots_of(gi)
        return [l0_parts(sl[k], t0 + k, False, seq, k) for k in range(2)]

    cache_state = {"issued": []}

    def sample_slot():
        return (2 * len(groups)) % 4

    def load_sample_x(sl):
        S.add("pool", lambda e: e.memset(xg[:, sl, :], 0.0), r=[("xg", sl)], w=[("xg", sl)])
        dma("sp", xg[0:16, sl, :], xsm_d.ap(), r=[("xg", sl)], w=[("xg", sl)])

    def cache_hook(c):
        for (j, slot) in cache_state["issued"]:
            def trk(e, slot=slot):
                ins = None
                for a in range(4):
                    ins = e.transpose(pT_[:, a * 128:(a + 1) * 128], qkb[:, slot * 512 + a * 128:slot * 512 + (a + 1) * 128], idb[:, :])
                return ins
            S.add("pe", trk, r=[("qb", "kb")[slot], "idb"], w=["pT"])
            S.add("dve", lambda e, j=j: e.tensor_copy(KT[:, :, j * 128:(j + 1) * 128], pT_[:, 0:512].rearrange("p (h t) -> p h t", h=4)), r=["pT"], w=[("KT", j)])
        cache_state["issued"] = []
        if c < 8:
            for slot in range(2):
                j = 2 * c + slot
                dma("pool", VA[:, j, :, 0:128], cv_d.ap()[j * 128:(j + 1) * 128, :].rearrange("p (h d) -> p h d", h=4), w=[("VA", j)])
                dma("pool", qkb[:, slot * 512:(slot + 1) * 512], ck_d.ap()[j * 128:(j + 1) * 128, :], w=[("qb", "kb")[slot]])
                cache_state["issued"].append((j, slot))

    parts = None
    if groups:
        parts = make_parts(0)
        emit_A(parts)
    late_setup(["bias"])
    mid_setup()
    late_setup(["ws", "wdw"])
    for gi in range(len(groups)):
        if gi + 1 < len(groups):
            load_group(gi + 1)
        for _ in range(6):
            if conv_thunks:
                conv_thunks.pop(0)()
        prefetch_w1()
        sl_ = slots_of(gi)
        pend = emit_rest(parts, after_first_O=lambda: l1_norm_a(sl_[0], 0), extra=conv_thunks)
        while conv_thunks:
            conv_thunks.pop(0)()
        seq, t0 = groups[gi]
        is_last = gi + 1 == len(groups)
        to_sample = is_last and do_sample
        if to_sample:
            load_sample_x(sample_slot())
            nxt_parts = [l0_parts(sample_slot(), 16, True, 0, 0)]
        else:
            nxt_parts = make_parts(gi + 1) if not is_last else None

        def early():
            if nxt_parts is not None:
                for p in nxt_parts:
                    p["A"][0]()
        p2 = l1_group(sl_, 2, False, seq, t0, first=(t0 == 0), last=(t0 + 2 >= n_prompt_tiles), pending_O=pend, w_prefetched=True, norm_a_done=(0,), early_hook=early,
                      per_c_hook=(cache_hook if to_sample else None))
        if p2 is None:
            if nxt_parts is not None:
                early()
        nxt = None
        if nxt_parts is not None:
            parts = nxt_parts
            nxt = [p["A"][1] for p in parts] + [p["A"][2] for p in parts]
        if p2 is None:
            if nxt:
                for f in nxt:
                    f()
            continue
        p2["stats"]()
        if nxt:
            for f in nxt[:len(parts)]:
                f()
        p2["norm"][0]()
        if nxt:
            nxt[len(parts)]()
        p2["norm"][1]()
        if nxt and len(parts) > 1:
            nxt[len(parts) + 1]()
        p2["tail"]()
    if do_sample and not groups:
        cache_state = {"issued": []}
        for c in range(9):
            cache_hook(c)
        sl = 0
        load_sample_x(sl)
        parts = [l0_parts(sl, 16, True, 0, 0)]
        emit_A(parts)
    if do_sample:
        pend = emit_rest(parts)
        p2 = l1_group([sample_slot()], 1, True, 0, 0, first=True, last=True, pending_O=pend)
        if p2 is not None:
            p2["stats"]()
            p2["norm"][0]()
            p2["norm"][1]()
            p2["tail"]()

    S.finish()
    S.emit(nc, st)
    st.close()
    return nc


def _t5_bucket_np(rel):
    import jax
    import jax.numpy as jnp
    cpu = jax.devices("cpu")[0]
    with jax.default_device(cpu):
        rel = jnp.asarray(rel, dtype=jnp.int32)
        nb = 16
        ret = jnp.where(rel > 0, nb, 0)
        n = jnp.abs(rel)
        max_exact = nb // 2
        nf = jnp.maximum(n, 1).astype(jnp.float32)
        large = max_exact + (jnp.log(nf / max_exact) / math.log(128 / max_exact) * (nb - max_exact)).astype(jnp.int32)
        large = jnp.minimum(large, nb - 1)
        out = ret + jnp.where(n < max_exact, n, large)
        return np.asarray(out)


def _consts():
    j = np.arange(384)
    rel = 127 - j
    b = _t5_bucket_np(rel)
    oh = np.zeros((32, 384), np.float32)
    oh[b, j] = 1.0
    oh[:, 383] = 0.0
    k = np.arange(128)[:, None]
    q = np.arange(128)[None, :]
    maskp = np.where((k // 64) <= (q // 64), 0.0, NEG).astype(np.float32)
    masks = np.where(k < 16, 0.0, NEG).astype(np.float32) + 0 * q
    blk = (np.arange(32)[None, :] == (np.arange(128)[:, None] % 32)).astype(np.float32)
    tri = (k <= q).astype(np.float32)
    idf = np.eye(128, dtype=np.float32)
    return {"c_oh": oh, "c_maskp": maskp, "c_masks": masks.astype(np.float32), "c_blk": blk, "c_tri": tri, "c_idf": idf}


_PROG = {}


def kernel(x_prompt, x_sample, cache_k0, cache_v0, state_conv1, rel_bias, norm_g0, w_in0,
           lambda_q1, lambda_k1, lambda_q2, lambda_k2, subln_g0, gv_ln_g0, gv_ln_b0, w_s0, b_s0,
           w_out0, norm_g1, w_in1, w_dw1, b_dw1, conv_ln_g1, conv_ln_b1, w_out1, final_g):
    f = lambda a: np.ascontiguousarray(np.asarray(a, dtype=np.float32))
    if "nc" not in _PROG:
        _PROG["nc"] = build_program()
    nc = _PROG["nc"]
    cst = _consts()
    shared = {
        "rel_bias": f(rel_bias), "norm_g0": f(norm_g0), "w_in0": f(w_in0),
        "lambda_q1": f(lambda_q1), "lambda_k1": f(lambda_k1), "lambda_q2": f(lambda_q2), "lambda_k2": f(lambda_k2),
        "subln_g0": f(subln_g0), "gv_ln_g0": f(gv_ln_g0).reshape(512), "gv_ln_b0": f(gv_ln_b0).reshape(512),
        "w_s0": f(w_s0), "b_s0": f(b_s0), "w_out0": f(w_out0), "norm_g1": f(norm_g1), "w_in1": f(w_in1),
        "w_dw1": f(w_dw1), "b_dw1": f(b_dw1), "conv_ln_g1": f(conv_ln_g1), "conv_ln_b1": f(conv_ln_b1),
        "w_out1": f(w_out1), "final_g": f(final_g),
    }
    shared.update(cst)
    xp = f(x_prompt)
    xsm = f(x_sample)
    ck = f(cache_k0).reshape(8, SEQ, 512)
    cv = f(cache_v0).reshape(8, SEQ, 512)
    sc = f(state_conv1)
    in_maps = []
    for c in range(NCORES):
        m = dict(shared)
        m["x"] = xp[2 * c:2 * c + 2]
        m["xsm"] = xsm[c]
        m["ck"] = ck[c]
        m["cv"] = cv[c]
        m["sc"] = sc[c]
        in_maps.append(m)
    res = run_bass_kernel_spmd(nc, in_maps, core_ids=list(range(NCORES)))
    R = res.results
    cat = lambda k: np.concatenate([np.asarray(r[k], dtype=np.float32) for r in R], axis=0)
    stk = lambda k: np.stack([np.asarray(r[k], dtype=np.float32) for r in R], axis=0)
    y_prompt = cat("y")
    y_sample = stk("ys")
    k0p = cat("ko").reshape(16, SEQ, 4, 128)
    v0p = cat("vo").reshape(16, SEQ, 4, 128)
    c1p = cat("c1p")
    k0s = stk("kso").reshape(8, 16, 4, 128)
    v0s = stk("vso").reshape(8, 16, 4, 128)
    gv0s = stk("gvso")
    c1s = stk("c1s")
    return (y_prompt, y_sample, k0p, v0p, c1p, k0s, v0s, gv0s, c1s)
```

```python
import math
from contextlib import ExitStack

import numpy as np
import ml_dtypes

import concourse.bass as bass
import concourse.mybir as mybir
from concourse.bass_utils import run_bass_kernel_spmd

F32 = mybir.dt.float32
BF16 = mybir.dt.bfloat16
AF = mybir.ActivationFunctionType
ALU = mybir.AluOpType
AX = mybir.AxisListType

NCORES = 8
D = 1024
SEQ = 2048
NTILE = SEQ // 128
NEG = -80000.0
GC = 0.7978845608028654


class Op:
    __slots__ = ("eng", "fn", "deps", "sem", "val", "needed", "dma", "semi")

    def __init__(self, eng, fn, dma):
        self.eng = eng
        self.fn = fn
        self.deps = set()
        self.sem = None
        self.val = 0
        self.needed = False
        self.dma = dma
        self.semi = None


class Sched:
    ENGS = ("pe", "act", "dve", "pool", "sp")
    EXCL = frozenset(["pA", "pB", "pT", "pS0", "pS1", "pO0", "pO1", "pM"])

    def __init__(self, n_dma_sems=None):
        n_dma_sems = n_dma_sems or {"sp": 16, "pool": 40, "pe": 1, "act": 1, "dve": 1}
        self.ops = {e: [] for e in self.ENGS}
        self.lastw = {}
        self.rd = {}
        self.ndma = n_dma_sems
        self.dma_rr = {e: 0 for e in self.ENGS}
        self.dma_last = {}
        self.dma_cnt = {}
        self.all_dma = []

    def add(self, eng, fn, r=(), w=(), dma=False):
        op = Op(eng, fn, dma)
        deps = set()
        for k in r:
            if k in self.lastw:
                deps.add(self.lastw[k])
            if k in self.EXCL:
                rdk = self.rd.get(k)
                if rdk:
                    deps.update(o for en, o in rdk[0].items() if en != eng)
                    deps.update(rdk[1])
        for k in w:
            if k in self.lastw:
                deps.add(self.lastw[k])
            rdk = self.rd.get(k)
            if rdk:
                deps.update(rdk[0].values())
                deps.update(rdk[1])
        for k in r:
            rdk = self.rd.setdefault(k, ({}, []))
            if dma:
                rdk[1].append(op)
            else:
                rdk[0][eng] = op
        for k in w:
            self.lastw[k] = op
            self.rd[k] = ({}, [])
        if dma:
            i = self.dma_rr[eng]
            self.dma_rr[eng] = (i + 1) % self.ndma[eng]
            key = (eng, i)
            op.semi = key
            if key in self.dma_last:
                deps.add(self.dma_last[key])
            self.dma_last[key] = op
            self.dma_cnt[key] = self.dma_cnt.get(key, 0) + 1
            op.val = 16 * self.dma_cnt[key]
            op.needed = True
            self.all_dma.append(op)
        deps.discard(op)
        op.deps = {d for d in deps if d.dma or not (d.eng == "pe" and eng == "pe")}
        for d in op.deps:
            d.needed = True
        self.ops[eng].append(op)
        return op

    def finish(self):
        for e in ("sp", "pool"):
            op = Op(e, None, False)
            op.deps = set(self.all_dma)
            self.ops[e].append(op)

    def emit(self, nc, stack):
        sems = {}
        for e in self.ENGS:
            sems[e] = stack.enter_context(nc.semaphore("s_" + e))
        dsems = {}
        for key in self.dma_cnt:
            dsems[key] = stack.enter_context(nc.semaphore("d_%s%d" % key))
        for e in self.ENGS:
            c = 0
            for op in self.ops[e]:
                if op.dma:
                    op.sem = dsems[op.semi]
                else:
                    op.sem = sems[e]
                    if op.needed:
                        c += 1
                        op.val = c
        block = stack.enter_context(nc.Block())

        def run(ename, eng):
            waited = {}
            for op in self.ops[ename]:
                for d in sorted(op.deps, key=lambda d: d.val):
                    k = id(d.sem)
                    if waited.get(k, 0) < d.val:
                        eng.wait_ge(d.sem, d.val)
                        waited[k] = d.val
                if op.fn is None:
                    continue
                ins = op.fn(eng)
                if op.dma:
                    ins.then_inc(op.sem, 16)
                elif op.needed:
                    ins.then_inc(op.sem, 1)

        block.tensor(lambda e: run("pe", e))
        block.scalar(lambda e: run("act", e))
        block.vector(lambda e: run("dve", e))
        block.gpsimd(lambda e: run("pool", e))
        block.sync(lambda e: run("sp", e))


def build_program(n_prompt_tiles=NTILE, do_sample=True, nseq=2):
    nc = bass.Bass("TRN2", target_bir_lowering=False)
    S = Sched()
    st = ExitStack()

    def din(name, shape, dt=F32):
        return nc.dram_tensor(name, shape, dt, kind="ExternalInput")

    def dout(name, shape):
        return nc.dram_tensor(name, shape, F32, kind="ExternalOutput")

    x_d = din("x", [2, SEQ, D])
    xsm_d = din("xsm", [16, D])
    ck_d = din("ck", [SEQ, 512])
    cv_d = din("cv", [SEQ, 512])
    sc_d = din("sc", [30, D])
    rb_d = din("rel_bias", [32, 4])
    g0_d = din("norm_g0", [D])
    win0_d = din("w_in0", [D, 3584])
    lq1_d = din("lambda_q1", [64])
    lk1_d = din("lambda_k1", [64])
    lq2_d = din("lambda_q2", [64])
    lk2_d = din("lambda_k2", [64])
    subg_d = din("subln_g0", [128])
    gvg_d = din("gv_ln_g0", [512])
    gvb_d = din("gv_ln_b0", [512])
    ws_d = din("w_s0", [4, 128, 128])
    bs_d = din("b_s0", [4, 128])
    wout0_d = din("w_out0", [D, D])
    g1_d = din("norm_g1", [D])
    win1_d = din("w_in1", [D, 3072])
    wdw_d = din("w_dw1", [31, D])
    bdw_d = din("b_dw1", [D])
    lng_d = din("conv_ln_g1", [D])
    lnb_d = din("conv_ln_b1", [D])
    wout1_d = din("w_out1", [D, D])
    fg_d = din("final_g", [D])
    coh_d = din("c_oh", [32, 384])
    cmp_d = din("c_maskp", [128, 128])
    cms_d = din("c_masks", [128, 128])
    cblk_d = din("c_blk", [128, 32])
    ctri_d = din("c_tri", [128, 128])
    cid_d = din("c_idf", [128, 128])

    y_d = dout("y", [2, SEQ, D])
    ys_d = dout("ys", [16, D])
    ko_d = dout("ko", [2, SEQ, 512])
    vo_d = dout("vo", [2, SEQ, 512])
    c1p_d = dout("c1p", [2, 30, D])
    kso_d = dout("kso", [16, 512])
    vso_d = dout("vso", [16, 512])
    gvso_d = dout("gvso", [16, 512])
    c1s_d = dout("c1s", [30, D])

    gd_d = nc.dram_tensor("gd_scr", [4, 128, 384], F32)
    w1s_d = nc.dram_tensor("w1s_scr", [8, 128, 3072], BF16)
    wo1s_d = nc.dram_tensor("wo1s_scr", [4, 128, 2048], BF16)

    def sb(name, shape, dt):
        return st.enter_context(nc.sbuf_tensor(name, shape, dt))

    def ps(name, shape, dt):
        return st.enter_context(nc.psum_tensor(name, shape, dt))

    w0 = sb("w0", [128, 8, 3584], BF16)
    wo0 = sb("wo0", [128, 8, 1024], BF16)
    KT = sb("KT", [128, 4, 17 * 128], BF16)
    VA = sb("VA", [128, 17, 4, 130], BF16)
    w1c = sb("w1c", [128, 2, 3072], BF16)
    wo1c = sb("wo1c", [128, 2, 2048], BF16)
    dgc = sb("dgc", [128, 2, 31, 32], BF16)
    blk31 = sb("blk31", [128, 32], F32)
    wdwT = sb("wdwT", [128, 8, 31], F32)
    idb = sb("idb", [128, 128], BF16)
    idf = sb("idf", [128, 128], F32)
    onesb = sb("onesb", [128, 128], BF16)
    BT = sb("BT", [128, 12, 128], BF16)
    wmT = sb("wmT", [128, 4, 128], BF16)
    cols = sb("cols", [128, 8, 8], F32)
    bsT = sb("bsT", [128, 4], F32)
    fgb = sb("fgb", [128, 1024], F32)
    gvgb = sb("gvgb", [128, 512], F32)
    gvbb = sb("gvbb", [128, 512], F32)
    subgb = sb("subgb", [128, 128], F32)
    nhalf = sb("nhalf", [128, 8], F32)
    stt = sb("stt", [128, 64], F32)
    xg = sb("xg", [128, 4, 1024], F32)
    xs = sb("xs", [128, 1024], BF16)
    xnT = sb("xnT", [128, 2, 8, 128], BF16)
    qkb = sb("qkb", [128, 1024], BF16)
    qT = sb("qT", [128, 2, 4, 128], BF16)
    T = sb("T", [128, 6, 512], F32)
    gates = sb("gates", [128, 4, 512], BF16)
    PT = sb("PT", [128, 2, 2176], BF16)
    uT = sb("uT", [128, 8, 286], BF16)
    yb = sb("yb", [128, 8, 256], BF16)
    ysq = sb("ysq", [128, 2, 256], BF16)
    junk = sb("junk", [128, 512], BF16)
    uo = junk[:, :].bitcast(F32).rearrange("p (c n) -> p c n", c=8)
    pA = ps("pA", [128, 512], F32)
    pB = ps("pB", [128, 512], F32)
    pT_ = ps("pT", [128, 1024], BF16)
    pS = [ps("pS0", [128, 512], F32), ps("pS1", [128, 512], F32)]
    pS3 = None
    pO = [ps("pO0", [128, 512], F32), ps("pO1", [128, 512], F32)]
    pM = ps("pM", [128, 512], F32)
    pS3 = [(pS[0], "pS0"), (pS[1], "pS1"), (pM, "pM"), (pT_[:, :].bitcast(F32), "pT")]

    def PTK(b):
        return [("PT", b, k) for k in range(5)]

    def dma(eng, out, in_, r=(), w=()):
        return S.add(eng, lambda e, o=out, i=in_: e.dma_start(out=o, in_=i), r=r, w=w, dma=True)

    def dma_nc(eng, out, in_, r=(), w=()):
        return S.add(eng, lambda e, o=out, i=in_: e.dma_start(out=o, in_=i, allow_slow_non_contiguous=True), r=r, w=w, dma=True)

    def bcast_rows(t, n, parts=128, off=0):
        return bass.AP(t, off, [[0, parts], [1, n]])

    import os
    _skip = set(os.environ.get("KSKIP", "").split(","))
    if os.environ.get("KSTOP") != "setup" and nseq > 0 and n_prompt_tiles >= 2:
        for k in range(2):
            dma("sp", xg[:, k, :], x_d.ap()[0, k * 128:(k + 1) * 128, :], w=[("xg", k)])
    dma("sp", idf[:, :], cid_d.ap(), w=["idf"])
    S.add("dve", lambda e: e.tensor_copy(idb[:, :], idf[:, :]), r=["idf"], w=["idb"])
    S.add("pool", lambda e: e.memset(onesb[:, :], 1.0), w=["onesb"])
    S.add("pool", lambda e: e.memset(nhalf[:, :], -0.5), w=["nhalf"])
    S.add("pool", lambda e: e.memset(stt[:, 56:57], 1e-5), w=["eps5"])
    S.add("pool", lambda e: e.memset(VA[:, :, :, 128:129], 1.0), w=["VAc"])
    S.add("pool", lambda e: e.memset(VA[:, :, :, 129:130], 0.0), w=["VAc2"])
    dma_nc("sp", cols[:, 0, :], g0_d.ap().rearrange("(c p) -> p c", p=128), w=[("cols", 0)])

    def mid_setup():
        for k, t in enumerate([g0_d, g1_d, bdw_d, lng_d, lnb_d]):
            if k > 0:
                dma_nc("sp", cols[:, k, :], t.ap().rearrange("(c p) -> p c", p=128), w=[("cols", k)])
        S.add("dve", lambda e: e.tensor_copy(cols[:, 5:7, :], cols[:, 3:5, :]), r=[("cols", 3), ("cols", 4)], w=[("cols", 5), ("cols", 6)])
        S.add("dve", lambda e: e.tensor_scalar(cols[:, 3:5, :], cols[:, 5:7, :], 0.5, None, ALU.mult), r=[("cols", 5), ("cols", 6)], w=[("cols", 3), ("cols", 4)])
        S.add("dve", lambda e: e.tensor_scalar(cols[:, 5:7, :], cols[:, 5:7, :], 0.25, None, ALU.mult), r=[("cols", 3), ("cols", 4), ("cols", 5), ("cols", 6)], w=[("cols", 5), ("cols", 6)])
        if "bcast" not in _skip:
            dma_nc("sp", bsT[:, :], bs_d.ap().rearrange("g i -> i g"), w=["bsT"])
            S.add("dve", lambda e: e.tensor_scalar(bsT[:, :], bsT[:, :], 0.25, None, ALU.mult), r=["bsT"], w=["bsT"])
            dma("sp", fgb[:, :], bcast_rows(fg_d, 1024), w=["fgb"])
            dma("sp", gvgb[:, :], bcast_rows(gvg_d, 512), w=["gvgb"])
            dma("sp", gvbb[:, :], bcast_rows(gvb_d, 512), w=["gvbb"])
            dma("sp", subgb[:, :], bcast_rows(subg_d, 128), w=["subgb"])
            S.add("dve", lambda e: e.tensor_scalar(subgb[:, :], subgb[:, :], 0.4, None, ALU.mult), r=["subgb"], w=["subgb"])
        lamt = T[:, 4, 0:256].rearrange("p (a b) -> p a b", a=4)
        for k, t in enumerate([lq1_d, lk1_d, lq2_d, lk2_d]):
            dma("sp", lamt[:, k, :], bcast_rows(t, 64), w=[("T", 4)])
        S.add("dve", lambda e: e.tensor_tensor(lamt[:, 0, :], lamt[:, 0, :], lamt[:, 1, :], ALU.mult), r=[("T", 4)], w=[("T", 4)])
        S.add("dve", lambda e: e.tensor_tensor(lamt[:, 2, :], lamt[:, 2, :], lamt[:, 3, :], ALU.mult), r=[("T", 4)], w=[("T", 4)])
        S.add("dve", lambda e: e.tensor_reduce(stt[:, 0:1], lamt[:, 0, :], AX.X, ALU.add), r=[("T", 4)], w=["lam0"])
        S.add("dve", lambda e: e.tensor_reduce(stt[:, 1:2], lamt[:, 2, :], AX.X, ALU.add), r=[("T", 4)], w=["lam1"])
        S.add("act", lambda e: e.activation(out=stt[:, 2:4], in_=stt[:, 0:2], func=AF.Exp), r=["lam0", "lam1"], w=["lam2"])
        S.add("dve", lambda e: e.tensor_tensor(stt[:, 4:5], stt[:, 3:4], stt[:, 2:3], ALU.subtract), r=["lam2"], w=["lam3"])
        S.add("dve", lambda e: e.tensor_scalar(stt[:, 5:6], stt[:, 4:5], -0.2, None, ALU.add), r=["lam3"], w=["nlam"])

    NLAM = stt[:, 5:6]

    def late_setup(which):
        if "bias" in which and "bias" not in _skip:
            rbt = T[0:32, 0, 0:4]
            rb15 = T[0:32, 0, 4:8]
            rb8 = T[0:32, 0, 8:12]
            oh = T[0:32, 1, 0:384]
            lhb = T[0:32, 2, :]
            dma("sp", rbt, rb_d.ap(), w=[("T", 0)])
            dma("sp", rb15, bass.AP(rb_d, 15 * 4, [[0, 32], [1, 4]]), w=[("T", 0)])
            dma("sp", oh, coh_d.ap(), w=[("T", 1)])
            S.add("dve", lambda e: e.tensor_tensor(rb8, rbt, rb15, ALU.subtract), r=[("T", 0), ("T", 0)], w=[("T", 0)])
            S.add("dve", lambda e: e.tensor_scalar(rb8, rb8, 8.0, None, ALU.mult), r=[("T", 0)], w=[("T", 0)])
            for h in range(4):
                S.add("dve", lambda e, h=h: e.tensor_copy(lhb[:, h * 128:(h + 1) * 128], rb8[:, h:h + 1].to_broadcast([32, 128])), r=[("T", 0)], w=[("T", 2)])
            for h in range(4):
                pbank = pA if h % 2 == 0 else pB
                pk = "pA" if h % 2 == 0 else "pB"
                S.add("pe", lambda e, h=h, pb=pbank: e.matmul(pb[:, 0:384], lhb[:, h * 128:(h + 1) * 128], oh, start=True, stop=True), r=[("T", 2), ("T", 1)], w=[pk])
                gsb = T[:, 3 + (h % 2), 0:384]
                gk = ("T", 3 + (h % 2))
                S.add("dve", lambda e, pb=pbank, g=gsb: e.tensor_copy(g, pb[:, 0:384]), r=[pk], w=[gk])
                dma("sp", gd_d.ap()[h], gsb, r=[gk], w=[("gd", h)])
                tdf = T[:, 5, 0:128]
                tpf = T[:, 5, 128:256]
                dma("sp", tdf, bass.AP(gd_d, h * 128 * 384 + 127, [[383, 128], [1, 128]]), r=[("gd", h)], w=[("T", 5)])
                dma("sp", tpf, bass.AP(gd_d, h * 128 * 384 + 255, [[383, 128], [1, 128]]), r=[("gd", h)], w=[("T", 5)])
                if h == 0:
                    dma("sp", T[:, 5, 256:384], cmp_d.ap(), w=[("T", 5)])
                    dma("sp", T[:, 5, 384:512], cms_d.ap(), w=[("T", 5)])
                S.add("dve", lambda e, h=h, a=tdf: e.tensor_tensor(BT[:, h, :], a, T[:, 5, 256:384], ALU.add), r=[("T", 5), ("T", 5)], w=[("BT", h)])
                S.add("dve", lambda e, h=h, a=tdf: e.tensor_tensor(BT[:, 8 + h, :], a, T[:, 5, 384:512], ALU.add), r=[("T", 5), ("T", 5)], w=[("BT", 8 + h)])
                S.add("dve", lambda e, h=h, a=tpf: e.tensor_copy(BT[:, 4 + h, :], a), r=[("T", 5)], w=[("BT", 4 + h)])

        if "ws" in which and "ws" not in _skip:
            dma("sp", T[:, 0, 0:128], ctri_d.ap(), r=[("T", 0), ("T", 0), ("T", 0)], w=[("T", 0)])
            for g in range(4):
                wsl = T[:, 1 + (g % 2), 0:128]
                wk = ("T", 1 + (g % 2))
                dma("sp", wsl, ws_d.ap()[g], r=[("T", 1), ("T", 2), ("T", 2), ("T", 2), ("T", 2)], w=[wk])
                S.add("pe", lambda e, a=wsl: e.transpose(pM[:, 0:128], a, idf[:, :]), r=[wk, "idf"], w=["pM"])
                S.add("dve", lambda e, g=g: e.scalar_tensor_tensor(wmT[:, g, :], pM[:, 0:128], 0.25, T[:, 0, 0:128], ALU.mult, ALU.mult), r=["pM", ("T", 0)], w=[("wmT", g)])

        if "wdw" in which and "wdw" not in _skip:
            dma("sp", T[0:31, 3, :], wdw_d.ap()[:, 0:512], r=[("T", 3)], w=[("T", 3)])
            dma("sp", T[0:31, 4, :], wdw_d.ap()[:, 512:1024], r=[("T", 4)], w=[("T", 4)])
            for c in range(8):
                src = T[0:31, 3 + c // 4, (c % 4) * 128:(c % 4 + 1) * 128]
                S.add("pe", lambda e, a=src: e.transpose(pM[:, 0:31], a, idf[0:31, 0:31]), r=[("T", 3 + c // 4), "idf"], w=["pM"])
                S.add("dve", lambda e, c=c: e.tensor_scalar(wdwT[:, c, :], pM[:, 0:31], 0.5, None, ALU.mult), r=["pM"], w=[("wdwT", c)])
            dma("sp", blk31[:, :], cblk_d.ap(), w=["blk31"])


    if "weights" not in _skip:
        w0v = win0_d.ap().rearrange("(dt p) e -> p dt e", p=128)
        for j in [1, 2, 0, 3, 4, 5, 6]:
            dma("pool", w0[:, :, j * 512:(j + 1) * 512], w0v[:, :, j * 512:(j + 1) * 512], w=[("w0", j)])
        wo0v = wout0_d.ap().rearrange("(dt p) e -> p dt e", p=128)
        for j in range(2):
            dma("pool", wo0[:, :, j * 512:(j + 1) * 512], wo0v[:, :, j * 512:(j + 1) * 512], w=[("wo0", j)])
    def convert_l1_weights():
        out = []
        w1v = win1_d.ap().rearrange("(dt p) (s e) -> p dt s e", p=128, s=3)
        for c in range(8):
            dst = w1s_d.ap()[c].rearrange("p (dt s e) -> p dt s e", dt=8, s=3)
            for s_ in range(3):
                out.append(lambda c=c, s_=s_, dst=dst: dma("pool", dst[:, :, s_, :], w1v[:, :, s_, c * 128:(c + 1) * 128], w=[("w1s", c, s_)]))
        wo1v = wout1_d.ap().rearrange("(ct p) e -> p ct e", p=128)
        for j in range(4):
            out.append(lambda j=j: dma("pool", wo1s_d.ap()[j].rearrange("p (ct e) -> p ct e", ct=8), wo1v[:, :, j * 256:(j + 1) * 256], w=[("wo1s", j)]))
        return out

    conv_thunks = convert_l1_weights() if "w1s" not in _skip else []

    state = {"xslot": 0, "pbank": 0, "sbank": 0, "obank": 0}
    KL0 = int(os.environ.get("KL0", "9"))
    KL1 = int(os.environ.get("KL1", "9"))

    def next_pbank():
        b = state["pbank"]
        state["pbank"] = 1 - b
        return (pA, "pA") if b == 0 else (pB, "pB")

    def rstd_from_sumsq(ss_ap, out_ap, n, eps, rk, wk, rks=None):
        S.add("pool", lambda e: e.tensor_scalar(out_ap, ss_ap, 1.0 / n, eps, ALU.mult, ALU.add), r=(rks if rks is not None else [rk]), w=[wk])
        S.add("pool", lambda e: e.tensor_tensor(out_ap, out_ap, nhalf[:, 0:out_ap.shape[1]], ALU.pow), r=[wk, "nhalf"], w=[wk])

    def norm_a(xin, xk, scr=None, scr_keys=None, sidx=0):
        cb = {0: 8, 1: 10, 2: 12, 3: 44}[sidx]
        if scr is None:
            scr, scr_keys = xs[:, :], ["xsL", "xsH"]
        ssk, rsk = ("ss", sidx), ("rstd", sidx)
        ssc = stt[:, cb:cb + 1]
        rsc = stt[:, cb + 1:cb + 2]
        S.add("act", lambda e: e.activation(out=scr, in_=xin, func=AF.Square, accum_out=ssc), r=[xk], w=[ssk] + scr_keys)
        rstd_from_sumsq(ssc, rsc, 1024.0, 1e-6, ssk, rsk)
        S.add("dve", lambda e: e.tensor_scalar(scr, xin, rsc, None, ALU.mult), r=[xk, rsk], w=scr_keys)

    def norm_b(gcol_k, dstT, dst_keys, scr=None, scr_keys=None):
        if scr is None:
            scr, scr_keys = xs[:, :], ["xsL", "xsH"]

        def tr(e):
            ins = None
            for dt in range(8):
                ins = e.transpose(pT_[:, dt * 128:(dt + 1) * 128], scr[:, dt * 128:(dt + 1) * 128], idb[:, :])
            return ins
        S.add("pe", tr, r=scr_keys + ["idb"], w=["pT"])
        S.add("dve", lambda e: e.tensor_tensor(dstT, pT_[:, :].rearrange("p (a b) -> p a b", a=8), cols[:, gcol_k, :].unsqueeze(2).to_broadcast([128, 8, 128]), ALU.mult), r=["pT", ("cols", gcol_k)], w=dst_keys)

    def norm_transpose(xin, xk, gcol_k, dstT, dst_keys, tagbase, scr=None, scr_keys=None, sidx=0):
        norm_a(xin, xk, scr, scr_keys, sidx)
        norm_b(gcol_k, dstT, dst_keys, scr, scr_keys)

    def l1_scr(k):
        return yb[:, 4 * k:4 * k + 4, :].rearrange("p a b -> p (a b)"), [("yb", 4 * k + i) for i in range(4)]

    def l1_norm_a(xslot, k):
        scr, sk = l1_scr(k)
        norm_a(xg[:, xslot, :], ("xg", xslot), scr, sk, 1 + k)

    def gate2(psrc, pk, dst, dk, tslot):
        tk = ("T", tslot)
        S.add("act", lambda e: e.activation(out=T[:, tslot, :], in_=psrc, func=AF.Tanh, scale=0.5), r=[pk], w=[tk])
        S.add("dve", lambda e: e.scalar_tensor_tensor(dst, T[:, tslot, :], 1.0, psrc, ALU.add, ALU.mult), r=[tk, pk], w=[dk])

    def gelu2_front(psrc, pk, ta, tb):
        ka, kb = ("T", ta), ("T", tb)
        S.add("act", lambda e: e.activation(out=T[:, ta, :], in_=psrc, func=AF.Copy), r=[pk], w=[ka])
        S.add("act", lambda e: e.activation(out=T[:, tb, :], in_=psrc, func=AF.Square), r=[pk], w=[kb])
        S.add("pool", lambda e: e.tensor_scalar(T[:, tb, :], T[:, tb, :], 0.044715, 1.0, ALU.mult, ALU.add), r=[kb], w=[kb])
        S.add("pool", lambda e: e.tensor_tensor(T[:, tb, :], T[:, tb, :], T[:, ta, :], ALU.mult), r=[ka, kb], w=[kb])

    def gelu2_back(ta, tb, dst, dk):
        ka, kb = ("T", ta), ("T", tb)
        S.add("act", lambda e: e.activation(out=T[:, tb, :], in_=T[:, tb, :], func=AF.Tanh, scale=GC), r=[kb], w=[kb])
        S.add("dve", lambda e: e.scalar_tensor_tensor(dst, T[:, tb, :], 1.0, T[:, ta, :], ALU.add, ALU.mult), r=[ka, kb], w=[dk])

    def l0_parts(xslot, ti, sample, seq, par):
        xin = xg[:, xslot, :]
        xk = ("xg", xslot)
        XT = xnT[:, par, :, :]
        XTK = ("xnT", par)
        QT = qT[:, par, :, :]
        QTK = ("qT", par)
        SGA, SGB, ZU, ZVB = gates[:, 0, :], gates[:, 1, :], gates[:, 2, :], gates[:, 3, :]
        ntk = ti + 1

        def proj(j):
            pb, pk = next_pbank()

            def mm(e):
                ins = None
                for dt in range(8):
                    ins = e.matmul(pb[:, :], XT[:, dt, :], w0[:, dt, j * 512:(j + 1) * 512], start=(dt == 0), stop=(dt == 7))
                return ins
            S.add("pe", mm, r=[XTK, ("w0", j)], w=[pk])
            return pb, pk

        if par == 0:
            a_scr, a_keys, a_sidx = xs[:, :], ["xsL", "xsH"], 0
        else:
            a_scr, a_keys, a_sidx = qkb[:, :], ["qb", "kb"], 3

        def A0():
            norm_a(xin, xk, a_scr, a_keys, a_sidx)

        def A0b():
            norm_b(0, XT, [XTK], a_scr, a_keys)

        def A1():
            pb, pk = proj(1)
            S.add("act", lambda e, pb=pb: e.activation(out=T[:, 0, :], in_=pb[:, :], func=AF.Copy), r=[pk], w=[("T", 0)])
            S.add("dve", lambda e, pb=pb: e.tensor_copy(qkb[:, 512:1024], pb[:, :]), r=[pk], w=["kb"])
            if sample:
                dma("sp", kso_d.ap(), T[0:16, 0, :], r=[("T", 0)])
            else:
                dma("sp", ko_d.ap()[seq, ti * 128:(ti + 1) * 128, :], T[:, 0, :], r=[("T", 0)])
            pb, pk = proj(2)
            S.add("act", lambda e, pb=pb: e.activation(out=T[:, 1, :], in_=pb[:, :], func=AF.Copy), r=[pk], w=[("T", 1)])
            S.add("dve", lambda e, pb=pb: e.tensor_copy(VA[:, ti, :, 0:128], pb[:, :].rearrange("p (h d) -> p h d", h=4)), r=[pk], w=[("VA", ti)])
            if sample:
                dma("sp", vso_d.ap(), T[0:16, 1, :], r=[("T", 1)])
            else:
                dma("sp", vo_d.ap()[seq, ti * 128:(ti + 1) * 128, :], T[:, 1, :], r=[("T", 1)])
            pb, pk = proj(0)
            S.add("act", lambda e, pb=pb: e.activation(out=qkb[:, 0:512], in_=pb[:, :], func=AF.Copy), r=[pk], w=["qb"])

            def trqk(e):
                ins = None
                for a in range(8):
                    ins = e.transpose(pT_[:, a * 128:(a + 1) * 128], qkb[:, a * 128:(a + 1) * 128], idb[:, :])
                return ins
            S.add("pe", trqk, r=["qb", "kb", "idb"], w=["pT"])
            S.add("dve", lambda e: e.tensor_copy(QT, pT_[:, 0:512].rearrange("p (h t) -> p h t", h=4)), r=["pT"], w=[QTK])
            S.add("dve", lambda e: e.tensor_copy(KT[:, :, ti * 128:(ti + 1) * 128], pT_[:, 512:1024].rearrange("p (h t) -> p h t", h=4)), r=["pT"], w=[("KT", ti)])

        def G0():
            pb, pk = proj(3)
            gate2(pb[:, :], pk, SGA, "sga", 2)

        def G1():
            pb, pk = proj(4)
            gelu2_front(pb[:, :], pk, 2, 3)

        def G1b():
            gelu2_back(2, 3, ZU, "zu")

        def G2():
            pb, pk = proj(5)
            gelu2_front(pb[:, :], pk, 0, 1)

        def G2b():
            gelu2_back(0, 1, T[:, 4, :], ("T", 4))

        def G2c():
            GV = T[:, 4, :]
            gv3 = GV.rearrange("p (g c) -> p g c", g=4)
            S.add("dve", lambda e: e.tensor_reduce(stt[:, 16:20], gv3, AX.X, ALU.add), r=[("T", 4)], w=["gvs"])
            S.add("pool", lambda e: e.tensor_tensor(T[:, 3, :], GV, GV, ALU.mult), r=[("T", 4)], w=[("T", 3)])
            S.add("dve", lambda e: e.tensor_reduce(stt[:, 20:24], T[:, 3, :].rearrange("p (g c) -> p g c", g=4), AX.X, ALU.add), r=[("T", 3)], w=["gvq"])
            S.add("pool", lambda e: e.tensor_scalar(stt[:, 16:20], stt[:, 16:20], 1.0 / 128, None, ALU.mult), r=["gvs"], w=["gvs"])
            S.add("pool", lambda e: e.tensor_tensor(stt[:, 24:28], stt[:, 16:20], stt[:, 16:20], ALU.mult), r=["gvs"], w=["gvm2"])
            S.add("dve", lambda e: e.scalar_tensor_tensor(stt[:, 20:24], stt[:, 20:24], 1.0 / 128, stt[:, 24:28], ALU.mult, ALU.subtract), r=["gvq", "gvm2"], w=["gvq"])
            S.add("pool", lambda e: e.tensor_scalar(stt[:, 20:24], stt[:, 20:24], 1.0, 4e-5, ALU.mult, ALU.add), r=["gvq"], w=["gvq"])
            S.add("pool", lambda e: e.tensor_tensor(stt[:, 20:24], stt[:, 20:24], nhalf[:, 0:4], ALU.pow), r=["gvq", "nhalf"], w=["gvq"])

        def G2d():
            GV = T[:, 4, :]
            for g in range(4):
                S.add("dve", lambda e, g=g: e.tensor_scalar(T[:, 3, g * 128:(g + 1) * 128], GV[:, g * 128:(g + 1) * 128], stt[:, 16 + g:17 + g], stt[:, 20 + g:21 + g], ALU.subtract, ALU.mult), r=[("T", 4), "gvs", "gvq"], w=[("T", 3)])
            S.add("pool", lambda e: e.tensor_tensor(T[:, 3, :], T[:, 3, :], gvgb[:, :], ALU.mult), r=[("T", 3), "gvgb"], w=[("T", 3)])
            S.add("pool", lambda e: e.tensor_tensor(T[:, 3, :], T[:, 3, :], gvbb[:, :], ALU.add), r=[("T", 3), "gvbb"], w=[("T", 3)])
            S.add("dve", lambda e: e.tensor_copy(ZVB, T[:, 3, :]), r=[("T", 3)], w=["zvb"])
            if sample:
                dma("sp", gvso_d.ap(), T[0:16, 3, :], r=[("T", 3)])

        def G3():
            pb, pk = proj(6)
            gate2(pb[:, :], pk, SGB, "sgb", 2)

        g4 = {}

        def G4():
            pb, pk = next_pbank()
            g4["pb"], g4["pk"] = pb, pk

            def spm(e):
                ins = None
                for g in range(4):
                    ins = e.matmul(pb[:, g * 128:(g + 1) * 128], wmT[:, g, :], ZVB[:, g * 128:(g + 1) * 128], start=True, stop=True)
                return ins
            S.add("pe", spm, r=["zvb"] + [("wmT", g) for g in range(4)], w=[pk])

        def G4b():
            pb, pk = g4["pb"], g4["pk"]
            for g in range(4):
                S.add("dve", lambda e, g=g: e.scalar_tensor_tensor(T[:, 3, g * 128:(g + 1) * 128], pb[:, g * 128:(g + 1) * 128], bsT[:, g:g + 1], ZU[:, g * 128:(g + 1) * 128], ALU.add, ALU.mult), r=[pk, "bsT", "zu", ("T", 3)], w=[("T", 3)])
            S.add("pool", lambda e: e.tensor_tensor(xs[:, 512:1024], T[:, 3, :], SGB, ALU.mult), r=[("T", 3), "sgb"], w=["xsH"])

        hc_state = {}

        def QK(h, c):
            ptb = (2 * h + c) % 2
            ptk = ("PT", ptb)
            for g0 in range(0, ntk, 4):
                g1 = min(ntk, g0 + 4)
                sbk = state["sbank"]
                state["sbank"] = (sbk + 1) % 4
                psb, psk = pS3[sbk]

                def qk(e, g0=g0, g1=g1, psb=psb):
                    ins = None
                    for j in range(g0, g1):
                        near = None
                        if j == ti:
                            near = (8 if sample else 0) + h
                        elif j == ti - 1:
                            near = 4 + h
                        o = psb[:, (j - g0) * 128:(j - g0 + 1) * 128]
                        ins = e.matmul(o, KT[c * 64:(c + 1) * 64, h, j * 128:(j + 1) * 128], QT[c * 64:(c + 1) * 64, h, :], start=True, stop=(near is None))
                        if near is not None:
                            ins = e.matmul(o, idb[:, :], BT[:, near, :], start=False, stop=True)
                    return ins
                S.add("pe", qk, r=[QTK, "idb"] + [("KT", j) for j in range(g0, g1)] + [("BT", k) for k in range(12)], w=[psk])
                S.add("act", lambda e, g0=g0, g1=g1, psb=psb: e.activation(out=PT[:, ptb, g0 * 128:g1 * 128], in_=psb[:, 0:(g1 - g0) * 128], func=AF.Exp, scale=0.125), r=[psk], w=[("PT", ptb, g0 // 4)])

        def PV(h, c):
            ptb = (2 * h + c) % 2
            ptk = ("PT", ptb)
            if c == 0:
                ob = state["obank"]
                state["obank"] = 1 - ob
                hc_state[h] = ob
            ob = hc_state[h]
            po, pok = pO[ob], "pO%d" % ob

            for g0 in range(0, ntk, 4):
                g1 = min(ntk, g0 + 4)

                def pv(e, g0=g0, g1=g1):
                    ins = None
                    for j in range(g0, g1):
                        ins = e.matmul(po[:, c * 130:(c + 1) * 130], PT[:, ptb, j * 128:(j + 1) * 128], VA[:, j, h, :], start=(j == 0), stop=(j == ntk - 1))
                    return ins
                S.add("pe", pv, r=[("PT", ptb, g0 // 4), "VAc", "VAc2"] + [("VA", j) for j in range(g0, g1)], w=[pok])

        def POST(h):
            ob = hc_state[h]
            po, pok = pO[ob], "pO%d" % ob
            S.add("dve", lambda e: e.reciprocal(stt[:, 32:34], po[:, 128:259:130]), r=[pok], w=["r01"])
            S.add("dve", lambda e: e.tensor_tensor(stt[:, 33:34], stt[:, 33:34], NLAM, ALU.mult), r=["r01", "nlam"], w=["r01"])
            oh_ = T[:, 5, h * 128:(h + 1) * 128]
            S.add("dve", lambda e: e.tensor_scalar(oh_, po[:, 0:128], stt[:, 32:33], None, ALU.mult), r=[pok, "r01"], w=[("T", 5)])
            S.add("dve", lambda e: e.scalar_tensor_tensor(oh_, po[:, 130:258], stt[:, 33:34], oh_, ALU.mult, ALU.add), r=[pok, "r01", ("T", 5)], w=[("T", 5)])
            S.add("act", lambda e: e.activation(out=junk[:, 0:128], in_=oh_, func=AF.Square, accum_out=stt[:, 36 + h:37 + h]), r=[("T", 5)], w=[("oss", h), "junk"])

        def FIN():
            rstd_from_sumsq(stt[:, 36:40], stt[:, 40:44], 128.0, 1e-6, None, "ors", rks=[("oss", h) for h in range(4)])
            o3 = T[:, 5, :].rearrange("p (h d) -> p h d", h=4)
            S.add("dve", lambda e: e.tensor_tensor(o3, o3, stt[:, 40:44].unsqueeze(2).to_broadcast([128, 4, 128]), ALU.mult), r=[("T", 5), "ors"], w=[("T", 5)])
            S.add("dve", lambda e: e.tensor_tensor(o3, o3, subgb[:, :].unsqueeze(1).to_broadcast([128, 4, 128]), ALU.mult), r=[("T", 5), "subgb"], w=[("T", 5)])
            S.add("pool", lambda e: e.tensor_tensor(xs[:, 0:512], T[:, 5, :], SGA, ALU.mult), r=[("T", 5), "sga"], w=["xsL"])

        def O0():
            def tro(e):
                ins = None
                for a in range(8):
                    ins = e.transpose(pT_[:, a * 128:(a + 1) * 128], xs[:, a * 128:(a + 1) * 128], idb[:, :])
                return ins
            S.add("pe", tro, r=["xsL", "xsH", "idb"], w=["pT"])
            S.add("dve", lambda e: e.tensor_copy(XT, pT_[:, :].rearrange("p (a b) -> p a b", a=8)), r=["pT"], w=[XTK])
            for j in range(2):
                pb, pk = next_pbank()

                def mm(e, pb=pb, j=j):
                    ins = None
                    for et in range(8):
                        ins = e.matmul(pb[:, :], XT[:, et, :], wo0[:, et, j * 512:(j + 1) * 512], start=(et == 0), stop=(et == 7))
                    return ins
                S.add("pe", mm, r=[XTK, ("wo0", j)], w=[pk])
                S.add("dve", lambda e, pb=pb, j=j: e.tensor_tensor(xin[:, j * 512:(j + 1) * 512], xin[:, j * 512:(j + 1) * 512], pb[:, :], ALU.add), r=[pk, xk], w=[xk])

        hcs = [(h, c) for h in range(4) for c in range(2)]
        H = []
        H.append(lambda: QK(0, 0))
        for n, (h, c) in enumerate(hcs):
            if n + 1 < 8:
                H.append(lambda n=n: QK(*hcs[n + 1]))
            H.append(lambda h=h, c=c: PV(h, c))
            if c == 1:
                H.append(lambda h=h: POST(h))
        H.append(FIN)
        return {"A": [A0, A0b, A1], "G": [G0, G1, G2, G1b, G2b, G2c, G2d, G3, G4, G4b], "H": H, "O": [O0]}

    def emit_A(parts):
        for p in parts:
            for f in p["A"]:
                f()

    def emit_rest(parts, after_first_O=None, extra=None):
        pending = None
        for p in parts:
            H, G = list(p["H"]), list(p["G"])
            G.pop(0)()
            for n, f in enumerate(H):
                f()
                if n == 2 and pending is not None:
                    pending()
                    pending = None
                    if after_first_O is not None:
                        after_first_O()
                        after_first_O = None
                if n % 2 == 1 and G:
                    G.pop(0)()
                if extra:
                    for _ in range(2):
                        if extra:
                            extra.pop(0)()
            while G:
                G.pop(0)()
            if pending is not None:
                pending()
            pending = p["O"][0]
        return pending

    def prefetch_w1():
        for c in range(2):
            dma("sp", w1c[:, c % 2, :], w1s_d.ap()[c], r=[("w1s", c, 0), ("w1s", c, 1), ("w1s", c, 2)], w=[("w1c", c % 2)])

    def l1_group(xslots, nt, sample, seq, tile0, first, last, pending_O=None, w_prefetched=False, norm_a_done=(), early_hook=None, per_c_hook=None):
        Tn = 128 * nt
        if KL1 <= 0:
            return
        XN = PT[:, 0, 0:2048].rearrange("p (a b) -> p a b", a=8)
        SG = PT[:, 1, 0:2048].rearrange("p (a b) -> p a b", a=8)
        if first and not sample:
            S.add("pool", lambda e: e.memset(uT[:, :, 0:30], 0.0), r=[("uT", c) for c in range(8)], w=[("uTh", c) for c in range(8)])
        elif sample:
            stv = T[0:30, 0:2, :].rearrange("p a b -> p (a b)")
            dma("sp", stv, sc_d.ap(), r=[("T", 0), ("T", 1)], w=[("T", 0), ("T", 1)])
            for c in range(8):
                S.add("pe", lambda e, c=c: e.transpose(pM[:, 0:30], stv[:, c * 128:(c + 1) * 128], idf[0:30, 0:30]), r=[("T", 0), ("T", 1), "idf"], w=["pM"])
                S.add("dve", lambda e, c=c: e.tensor_scalar(uT[:, c, 0:30], pM[:, 0:30], 2.0, None, ALU.mult), r=["pM", ("uT", c)], w=[("uTh", c)])
            dma("sp", c1s_d.ap()[0:14, :], sc_d.ap()[16:30, :])
        else:
            S.add("pool", lambda e: e.tensor_copy(uT[:, :, 0:30], uT[:, :, 256:286]), r=[("uT", c) for c in range(8)], w=[("uTh", c) for c in range(8)])
        for k, xslot in enumerate(xslots):
            if k == len(xslots) - 1 and pending_O is not None:
                pending_O()
            if k not in norm_a_done:
                l1_norm_a(xslot, k)
            scr, sk = l1_scr(k)
            norm_b(1, XN[:, :, k * 128:(k + 1) * 128], [*PTK(0)], scr, sk)

        if KL1 <= 1:
            return
        banks = [(pA, "pA", pB, "pB", pB[:, 256:512], "pB"), (pS[0], "pS0", pS[1], "pS1", pS[1][:, 256:512], "pS1")]

        def loadw(c):
            dma("sp", w1c[:, c % 2, :], w1s_d.ap()[c], r=[("w1s", c, 0), ("w1s", c, 1), ("w1s", c, 2)], w=[("w1c", c % 2)])

        def projc(c):
            pab, pabk, pg, pgk, _, _ = banks[c % 2]
            wv = w1c[:, c % 2, :].rearrange("p (dt s e) -> p dt s e", dt=8, s=3)

            def mm_ab(e):
                ins = None
                for s_, dst in ((1, pab[:, 256:256 + Tn]), (0, pab[:, 0:Tn])):
                    for dt in range(8):
                        ins = e.matmul(dst, wv[:, dt, s_, :], XN[:, dt, 0:Tn], start=(dt == 0), stop=(dt == 7))
                return ins

            def mm_g(e):
                ins = None
                for dt in range(8):
                    ins = e.matmul(pg[:, 0:Tn], wv[:, dt, 2, :], XN[:, dt, 0:Tn], start=(dt == 0), stop=(dt == 7))
                return ins
            S.add("pe", mm_ab, r=[("w1c", c % 2), *PTK(0)], w=[pabk])
            S.add("pe", mm_g, r=[("w1c", c % 2), *PTK(0)], w=[pgk])

        def dgbuild(c):
            S.add("pool", lambda e: e.tensor_tensor(dgc[:, c % 2, :, :], blk31[:, :].unsqueeze(1).to_broadcast([128, 31, 32]), wdwT[:, c, :].unsqueeze(2).to_broadcast([128, 31, 32]), ALU.mult), r=["blk31", ("wdwT", c)], w=[("dgc", c % 2)])

        def elem_u(c):
            pab, pabk, pg, pgk, _, _ = banks[c % 2]
            ta = T[:, 0, 0:Tn]
            S.add("act", lambda e: e.activation(out=ta, in_=pab[:, 256:256 + Tn], func=AF.Tanh, scale=0.5), r=[pabk], w=[("T", 0)])
            S.add("dve", lambda e: e.scalar_tensor_tensor(uT[:, c, 30:30 + Tn], ta, 1.0, pab[:, 0:Tn], ALU.add, ALU.mult), r=[("T", 0), pabk, ("uTh", c)], w=[("uT", c)])
            if last:
                n_o = 16 if sample else 30
                c0 = 0 if sample else Tn - 30
                S.add("dve", lambda e: e.scalar_tensor_tensor(uo[:, c, 0:n_o], ta[:, c0:c0 + n_o], 1.0, pab[:, c0:c0 + n_o], ALU.add, ALU.mult), r=[("T", 0), pabk], w=[("uo", c), "junk"])
            tg = T[:, 1, 0:Tn]
            S.add("act", lambda e: e.activation(out=tg, in_=pg[:, 0:Tn], func=AF.Tanh, scale=0.5), r=[pgk], w=[("T", 1)])
            S.add("dve", lambda e: e.scalar_tensor_tensor(SG[:, c, 0:Tn], tg, 1.0, pg[:, 0:Tn], ALU.add, ALU.mult), r=[("T", 1), pgk], w=[*PTK(1)])

        def conv(c):
            _, _, _, _, pc, pck = banks[c % 2]

            def mm(e):
                ins = None
                for w_ in range(31):
                    for b in range(4):
                        ins = e.matmul(pc[32 * b:32 * b + 32, 0:Tn], dgc[32 * b:32 * b + 32, c % 2, w_, :], uT[32 * b:32 * b + 32, c, w_:w_ + Tn], start=(w_ == 0), stop=(w_ == 30), tile_position=(32 * b, 32 * b))
                return ins
            S.add("pe", mm, r=[("dgc", c % 2), ("uT", c), ("uTh", c)], w=[pck])

        def evac_y(c):
            _, _, _, _, pc, pck = banks[c % 2]
            S.add("act", lambda e: e.activation(out=yb[:, c, 0:Tn], in_=pc[:, 0:Tn], func=AF.Identity, bias=cols[:, 2, c:c + 1]), r=[pck, ("cols", 2)], w=[("yb", c)])
            S.add("act", lambda e: e.activation(out=ysq[:, c % 2, 0:Tn], in_=pc[:, 0:Tn], func=AF.Square, bias=cols[:, 2, c:c + 1]), r=[pck, ("cols", 2)], w=[("ysq", c % 2)])

        def stats(c):
            def mm(e):
                e.matmul(pM[:, 0:Tn], onesb[:, :], yb[:, c, 0:Tn], start=(c == 0), stop=(c == 7))
                return e.matmul(pO[0][:, 0:Tn], onesb[:, :], ysq[:, c % 2, 0:Tn], start=(c == 0), stop=(c == 7))
            S.add("pe", mm, r=[("yb", c), ("ysq", c % 2), "onesb"], w=["pM", "pO0"])

        if not w_prefetched:
            loadw(0)
            loadw(1)
        if not state.get("dg_primed"):
            dgbuild(0)
            dgbuild(1)
            state["dg_primed"] = True
        projc(0)
        for c in range(8):
            if c == 5 and early_hook is not None:
                early_hook()
            if per_c_hook is not None:
                per_c_hook(c)
            if c + 1 < 8:
                projc(c + 1)
            elem_u(c)
            if c + 2 < 8:
                loadw(c + 2)
            conv(c)
            dgbuild((c + 2) % 8)
            evac_y(c)
            if c >= 1:
                stats(c - 1)
        stats(7)
        S.add("act", lambda e: e.activation(out=stt[:, 58:59], in_=stt[:, 56:57], func=AF.Sqrt), r=["eps5"], w=["sqdummy"])
        if per_c_hook is not None:
            per_c_hook(8)
        if KL1 <= 2:
            return
        if last:
            n_o = 16 if sample else 30
            for half in range(2):
                pbh, pbk = (pA, "pA") if half == 0 else (pB, "pB")

                def tru(e, half=half, pbh=pbh):
                    ins = None
                    for cc in range(4):
                        c = half * 4 + cc
                        ins = e.transpose(pbh[0:n_o, cc * 128:(cc + 1) * 128], uo[:, c, 0:n_o], idf[:, :])
                    return ins
                S.add("pe", tru, r=[("uo", c) for c in range(8)] + ["idf", "junk"], w=[pbk])
                S.add("dve", lambda e, half=half, pbh=pbh: e.tensor_scalar(T[0:n_o, 2 + half, :], pbh[0:n_o, :], 0.5, None, ALU.mult), r=[pbk], w=[("T", 2 + half)])
                if sample:
                    dma("sp", c1s_d.ap()[14:30, half * 512:(half + 1) * 512], T[0:16, 2 + half, :], r=[("T", 2 + half)])
                else:
                    dma("sp", c1p_d.ap()[seq, :, half * 512:(half + 1) * 512], T[0:30, 2 + half, :], r=[("T", 2 + half)])
        pt0f = PT[:, 0, 0:1536].bitcast(F32)
        M_, R_, Q_ = pt0f[:, 0:Tn], pt0f[:, 256:256 + Tn], pt0f[:, 512:512 + Tn]

        def p2_stats():
            pass
            S.add("act", lambda e: e.activation(out=M_, in_=pM[:, 0:Tn], func=AF.Copy, scale=1.0 / 1024), r=["pM"], w=[*PTK(0)])
            S.add("dve", lambda e: e.tensor_tensor(Q_, M_, M_, ALU.mult), r=[*PTK(0)], w=["mrQ"])
            S.add("dve", lambda e: e.scalar_tensor_tensor(R_, pO[0][:, 0:Tn], 1.0 / 1024, Q_, ALU.mult, ALU.subtract), r=["pO0", "mrQ"], w=[*PTK(0)])
            S.add("act", lambda e: e.activation(out=R_, in_=R_, func=AF.Sqrt, bias=stt[:, 56:57]), r=[*PTK(0), "eps5"], w=[*PTK(0)])
            S.add("act", lambda e: e.activation(out=stt[:, 59:60], in_=stt[:, 56:57], func=AF.Tanh), r=["eps5"], w=["thdummy"])
            S.add("dve", lambda e: e.reciprocal(R_, R_), r=[*PTK(0)], w=[*PTK(0)])

        def p2_norm(c0):
            if True:
                cs = list(range(c0, c0 + 4))
                sets = {c: (T[:, 2 + (c % 4), 0:Tn], T[:, 2 + (c % 4), 256:256 + Tn], ("T", 2 + (c % 4))) for c in cs}
                for c in cs:
                    t1, t2, k1 = sets[c]
                    S.add("dve", lambda e, c=c, t1=t1: e.tensor_tensor(t1, yb[:, c, 0:Tn], M_, ALU.subtract), r=[("yb", c), *PTK(0)], w=[k1])
                for c in cs:
                    t1, t2, k1 = sets[c]
                    S.add("pool", lambda e, t1=t1: e.tensor_tensor(t1, t1, R_, ALU.mult), r=[k1, *PTK(0)], w=[k1])
                for c in cs:
                    t1, t2, k1 = sets[c]
                    S.add("act", lambda e, c=c, t1=t1, t2=t2: e.activation(out=t2, in_=t1, func=AF.Tanh, scale=cols[:, 3, c:c + 1], bias=cols[:, 4, c:c + 1]), r=[k1, ("cols", 3), ("cols", 4)], w=[k1])
                    S.add("act", lambda e, c=c, t1=t1: e.activation(out=t1, in_=t1, func=AF.Identity, scale=cols[:, 5, c:c + 1], bias=cols[:, 6, c:c + 1]), r=[k1, ("cols", 5), ("cols", 6)], w=[k1])
                for c in cs:
                    t1, t2, k1 = sets[c]
                    S.add("dve", lambda e, t1=t1, t2=t2: e.scalar_tensor_tensor(t1, t2, 1.0, t1, ALU.add, ALU.mult), r=[k1], w=[k1])
                for c in cs:
                    t1, t2, k1 = sets[c]
                    S.add("pool", lambda e, c=c, t1=t1: e.tensor_tensor(yb[:, c, 0:Tn], t1, SG[:, c, 0:Tn], ALU.mult), r=[k1, *PTK(1)], w=[("yb", c)])

        def p2_tail():
            for j in range(4):
                dma("sp", wo1c[:, j % 2, :], wo1s_d.ap()[j], r=[("wo1s", j)], w=[("wo1c", j % 2)])
                pb, pk = next_pbank()
                wv = wo1c[:, j % 2, :].rearrange("p (ct e) -> p ct e", ct=8)

                def mm(e, pb=pb, wv=wv):
                    ins = None
                    for k in range(nt):
                        for ct in range(8):
                            ins = e.matmul(pb[:, k * 256:(k + 1) * 256], yb[:, ct, k * 128:(k + 1) * 128], wv[:, ct, :], start=(ct == 0), stop=(ct == 7))
                    return ins
                S.add("pe", mm, r=[("wo1c", j % 2)] + [("yb", c) for c in range(8)], w=[pk])
                for k, xslot in enumerate(xslots):
                    xv = xg[:, xslot, j * 256:(j + 1) * 256]
                    S.add("dve", lambda e, pb=pb, k=k, xv=xv: e.tensor_tensor(xv, xv, pb[:, k * 256:(k + 1) * 256], ALU.add), r=[pk, ("xg", xslot)], w=[("xg", xslot)])
            for k, xslot in enumerate(xslots):
                xv = xg[:, xslot, :]
                xk = ("xg", xslot)
                S.add("act", lambda e, xv=xv: e.activation(out=junk[:, :], in_=xv[:, 0:512], func=AF.Square, accum_out=stt[:, 48:49]), r=[xk], w=["fs0", "junk"])
                S.add("act", lambda e, xv=xv: e.activation(out=junk[:, :], in_=xv[:, 512:1024], func=AF.Square, accum_out=stt[:, 49:50]), r=[xk], w=["fs1", "junk"])
                S.add("pool", lambda e: e.tensor_tensor(stt[:, 50:51], stt[:, 48:49], stt[:, 49:50], ALU.add), r=["fs0", "fs1"], w=["fs"])
                rstd_from_sumsq(stt[:, 50:51], stt[:, 51:52], 1024.0, 1e-6, "fs", "frs")
                ov = T[:, 4:6, :].rearrange("p a b -> p (a b)")
                S.add("dve", lambda e, xv=xv, ov=ov: e.scalar_tensor_tensor(ov, xv, stt[:, 51:52], fgb[:, :], ALU.mult, ALU.mult), r=[xk, "frs", "fgb"], w=[("T", 4), ("T", 5)])
                if sample:
                    dma("sp", ys_d.ap(), ov[0:16, :], r=[("T", 4), ("T", 5)])
                else:
                    t = tile0 + k
                    dma("sp", y_d.ap()[seq, t * 128:(t + 1) * 128, :], ov, r=[("T", 4), ("T", 5)])


        return {"stats": p2_stats, "norm": [lambda: p2_norm(0), lambda: p2_norm(4)], "tail": p2_tail}

    def load_x(slot, seq, ti):
        dma("sp", xg[:, slot, :], x_d.ap()[seq, ti * 128:(ti + 1) * 128, :], w=[("xg", slot)])

    import os
    groups = []
    if os.environ.get("KSTOP") != "setup":
        for seq in range(nseq):
            for t0 in range(0, n_prompt_tiles, 2):
                groups.append((seq, t0))
    else:
        do_sample = False

    def slots_of(gi):
        return [(2 * gi) % 4, (2 * gi + 1) % 4]

    def load_group(gi):
        seq, t0 = groups[gi]
        sl = slots_of(gi)
        load_x(sl[0], seq, t0)
        load_x(sl[1], seq, t0 + 1)

    def make_parts(gi):
        seq, t0 = groups[gi]
        sl = slots_of(gi)
        return [l0_parts(sl[k], t0 + k, False, seq, k) for k in range(2)]

    cache_state = {"issued": []}

    def sample_slot():
        return (2 * len(groups)) % 4

    def load_sample_x(sl):
        S.add("pool", lambda e: e.memset(xg[:, sl, :], 0.0), r=[("xg", sl)], w=[("xg", sl)])
        dma("sp", xg[0:16, sl, :], xsm_d.ap(), r=[("xg", sl)], w=[("xg", sl)])

    def cache_hook(c):
        for (j, slot) in cache_state["issued"]:
            def trk(e, slot=slot):
                ins = None
                for a in range(4):
                    ins = e.transpose(pT_[:, a * 128:(a + 1) * 128], qkb[:, slot * 512 + a * 128:slot * 512 + (a + 1) * 128], idb[:, :])
                return ins
            S.add("pe", trk, r=[("qb", "kb")[slot], "idb"], w=["pT"])
            S.add("dve", lambda e, j=j: e.tensor_copy(KT[:, :, j * 128:(j + 1) * 128], pT_[:, 0:512].rearrange("p (h t) -> p h t", h=4)), r=["pT"], w=[("KT", j)])
        cache_state["issued"] = []
        if c < 8:
            for slot in range(2):
                j = 2 * c + slot
                dma("pool", VA[:, j, :, 0:128], cv_d.ap()[j * 128:(j + 1) * 128, :].rearrange("p (h d) -> p h d", h=4), w=[("VA", j)])
                dma("pool", qkb[:, slot * 512:(slot + 1) * 512], ck_d.ap()[j * 128:(j + 1) * 128, :], w=[("qb", "kb")[slot]])
                cache_state["issued"].append((j, slot))

    parts = None
    if groups:
        parts = make_parts(0)
        emit_A(parts)
    late_setup(["bias"])
    mid_setup()
    late_setup(["ws", "wdw"])
    for gi in range(len(groups)):
        if gi + 1 < len(groups):
            load_group(gi + 1)
        for _ in range(6):
            if conv_thunks:
                conv_thunks.pop(0)()
        prefetch_w1()
        sl_ = slots_of(gi)
        pend = emit_rest(parts, after_first_O=lambda: l1_norm_a(sl_[0], 0), extra=conv_thunks)
        while conv_thunks:
            conv_thunks.pop(0)()
        seq, t0 = groups[gi]
        is_last = gi + 1 == len(groups)
        to_sample = is_last and do_sample
        if to_sample:
            load_sample_x(sample_slot())
            nxt_parts = [l0_parts(sample_slot(), 16, True, 0, 0)]
        else:
            nxt_parts = make_parts(gi + 1) if not is_last else None

        def early():
            if nxt_parts is not None:
                for p in nxt_parts:
                    p["A"][0]()
        p2 = l1_group(sl_, 2, False, seq, t0, first=(t0 == 0), last=(t0 + 2 >= n_prompt_tiles), pending_O=pend, w_prefetched=True, norm_a_done=(0,), early_hook=early,
                      per_c_hook=(cache_hook if to_sample else None))
        if p2 is None:
            if nxt_parts is not None:
                early()
        nxt = None
        if nxt_parts is not None:
            parts = nxt_parts
            nxt = [p["A"][1] for p in parts] + [p["A"][2] for p in parts]
        if p2 is None:
            if nxt:
                for f in nxt:
                    f()
            continue
        p2["stats"]()
        if nxt:
            for f in nxt[:len(parts)]:
                f()
        p2["norm"][0]()
        if nxt:
            nxt[len(parts)]()
        p2["norm"][1]()
        if nxt and len(parts) > 1:
            nxt[len(parts) + 1]()
        p2["tail"]()
    if do_sample and not groups:
        cache_state = {"issued": []}
        for c in range(9):
            cache_hook(c)
        sl = 0
        load_sample_x(sl)
        parts = [l0_parts(sl, 16, True, 0, 0)]
        emit_A(parts)
    if do_sample:
        pend = emit_rest(parts)
        p2 = l1_group([sample_slot()], 1, True, 0, 0, first=True, last=True, pending_O=pend)
        if p2 is not None:
            p2["stats"]()
            p2["norm"][0]()
            p2["norm"][1]()
            p2["tail"]()

    S.finish()
    S.emit(nc, st)
    st.close()
    return nc


def _t5_bucket_np(rel):
    import jax
    import jax.numpy as jnp
    cpu = jax.devices("cpu")[0]
    with jax.default_device(cpu):
        rel = jnp.asarray(rel, dtype=jnp.int32)
        nb = 16
        ret = jnp.where(rel > 0, nb, 0)
        n = jnp.abs(rel)
        max_exact = nb // 2
        nf = jnp.maximum(n, 1).astype(jnp.float32)
        large = max_exact + (jnp.log(nf / max_exact) / math.log(128 / max_exact) * (nb - max_exact)).astype(jnp.int32)
        large = jnp.minimum(large, nb - 1)
        out = ret + jnp.where(n < max_exact, n, large)
        return np.asarray(out)


def _consts():
    j = np.arange(384)
    rel = 127 - j
    b = _t5_bucket_np(rel)
    oh = np.zeros((32, 384), np.float32)
    oh[b, j] = 1.0
    oh[:, 383] = 0.0
    k = np.arange(128)[:, None]
    q = np.arange(128)[None, :]
    maskp = np.where((k // 64) <= (q // 64), 0.0, NEG).astype(np.float32)
    masks = np.where(k < 16, 0.0, NEG).astype(np.float32) + 0 * q
    blk = (np.arange(32)[None, :] == (np.arange(128)[:, None] % 32)).astype(np.float32)
    tri = (k <= q).astype(np.float32)
    idf = np.eye(128, dtype=np.float32)
    return {"c_oh": oh, "c_maskp": maskp, "c_masks": masks.astype(np.float32), "c_blk": blk, "c_tri": tri, "c_idf": idf}


_PROG = {}


def kernel(x_prompt, x_sample, cache_k0, cache_v0, state_conv1, rel_bias, norm_g0, w_in0,
           lambda_q1, lambda_k1, lambda_q2, lambda_k2, subln_g0, gv_ln_g0, gv_ln_b0, w_s0, b_s0,
           w_out0, norm_g1, w_in1, w_dw1, b_dw1, conv_ln_g1, conv_ln_b1, w_out1, final_g):
    f = lambda a: np.ascontiguousarray(np.asarray(a, dtype=np.float32))
    if "nc" not in _PROG:
        _PROG["nc"] = build_program()
    nc = _PROG["nc"]
    cst = _consts()
    shared = {
        "rel_bias": f(rel_bias), "norm_g0": f(norm_g0), "w_in0": f(w_in0),
        "lambda_q1": f(lambda_q1), "lambda_k1": f(lambda_k1), "lambda_q2": f(lambda_q2), "lambda_k2": f(lambda_k2),
        "subln_g0": f(subln_g0), "gv_ln_g0": f(gv_ln_g0).reshape(512), "gv_ln_b0": f(gv_ln_b0).reshape(512),
        "w_s0": f(w_s0), "b_s0": f(b_s0), "w_out0": f(w_out0), "norm_g1": f(norm_g1), "w_in1": f(w_in1),
        "w_dw1": f(w_dw1), "b_dw1": f(b_dw1), "conv_ln_g1": f(conv_ln_g1), "conv_ln_b1": f(conv_ln_b1),
        "w_out1": f(w_out1), "final_g": f(final_g),
    }
    shared.update(cst)
    xp = f(x_prompt)
    xsm = f(x_sample)
    ck = f(cache_k0).reshape(8, SEQ, 512)
    cv = f(cache_v0).reshape(8, SEQ, 512)
    sc = f(state_conv1)
    in_maps = []
    for c in range(NCORES):
        m = dict(shared)
        m["x"] = xp[2 * c:2 * c + 2]
        m["xsm"] = xsm[c]
        m["ck"] = ck[c]
        m["cv"] = cv[c]
        m["sc"] = sc[c]
        in_maps.append(m)
    res = run_bass_kernel_spmd(nc, in_maps, core_ids=list(range(NCORES)))
    R = res.results
    cat = lambda k: np.concatenate([np.asarray(r[k], dtype=np.float32) for r in R], axis=0)
    stk = lambda k: np.stack([np.asarray(r[k], dtype=np.float32) for r in R], axis=0)
    y_prompt = cat("y")
    y_sample = stk("ys")
    k0p = cat("ko").reshape(16, SEQ, 4, 128)
    v0p = cat("vo").reshape(16, SEQ, 4, 128)
    c1p = cat("c1p")
    k0s = stk("kso").reshape(8, 16, 4, 128)
    v0s = stk("vso").reshape(8, 16, 4, 128)
    gv0s = stk("gvso")
    c1s = stk("c1s")
    return (y_prompt, y_sample, k0p, v0p, c1p, k0s, v0s, gv0s, c1s)
```
